# Optimizing a Trainium2 kernel written in Bass

```python
import math
import numpy as np
import jax
import jax.numpy as jnp
from jax import lax

D_MODEL = 1024
BATCH = 16
SEQ = 256
DEPTH = 4
DEC_BATCH = 2
DEC_SEQ = 1024
PAST_LEN = 512

GRID_W = 64
HEAD_DIM = 64
NA_HEADS = 4
NA_WIN_ROWS = 8
NA_WIN_COLS = 16
DIFF_HEADS = 4
DIFF_QK_DIM = 32
DIFF_V_DIM = 64
MLA_HEADS = 4
MLA_NOPE = 64
MLA_ROPE = 32
MLA_V = 64
MLA_KV_RANK = 128
SGU_GROUPS = 4
SGU_GROUP_DIM = 64
SGU_CHUNK = 128
D_FF = 4 * D_MODEL
ROPE_BASE = 10000.0
QBLOCK = 128
EPS = 1e-6
NEG_INF = -1e30

WIDTH_A = NA_HEADS * HEAD_DIM
WIDTH_B = DIFF_HEADS * DIFF_V_DIM
WIDTH_C = MLA_HEADS * MLA_V
WIDTH_D = SGU_GROUPS * SGU_GROUP_DIM
MIX_WIDTH = WIDTH_A + WIDTH_B + WIDTH_C + WIDTH_D
DIFF_QK_W = DIFF_HEADS * 2 * DIFF_QK_DIM
MLA_Q_W = MLA_HEADS * (MLA_NOPE + MLA_ROPE)
IN_SPLITS = (WIDTH_A, WIDTH_A, WIDTH_A, DIFF_QK_W, DIFF_QK_W, WIDTH_B,
             MLA_Q_W, MLA_KV_RANK, MLA_ROPE, WIDTH_D, WIDTH_D)
IN_COLS = 3 * WIDTH_A + 2 * DIFF_QK_W + WIDTH_B + MLA_Q_W + MLA_KV_RANK + MLA_ROPE + 2 * WIDTH_D

kernel_name = 'hybrid_na_diff_mla_sgu_dit_step'


def rmsnorm(x, g):
    xf = x.astype(jnp.float32)
    y = xf * lax.rsqrt(jnp.mean(xf * xf, axis=-1, keepdims=True) + EPS)
    return (y * g.astype(jnp.float32)).astype(x.dtype)


def _heads(x, n):
    b, s, _ = x.shape
    return x.reshape(b, s, n, -1).transpose(0, 2, 1, 3)


def _merge(x):
    b, h, s, d = x.shape
    return x.transpose(0, 2, 1, 3).reshape(b, s, h * d)


def _rope1d(x, pos):
    half = x.shape[-1] // 2
    freqs = ROPE_BASE ** (-jnp.arange(half, dtype=jnp.float32) / half)
    ang = pos[:, None] * freqs[None, :]
    cos, sin = jnp.cos(ang).astype(x.dtype), jnp.sin(ang).astype(x.dtype)
    x1, x2 = x[..., :half], x[..., half:]
    return jnp.concatenate([x1 * cos - x2 * sin, x2 * cos + x1 * sin], axis=-1)


def rope2d(x, rows, cols):
    r = x.shape[-1] // 2
    return jnp.concatenate([_rope1d(x[..., :r], rows), _rope1d(x[..., r:], cols)], axis=-1)


def _grid_positions(s):
    t = jnp.arange(s)
    return (t // GRID_W).astype(jnp.float32), (t % GRID_W).astype(jnp.float32)


def _map_query_blocks(fn, *qs):
    b, h, s = qs[0].shape[:3]
    nb = s // QBLOCK
    blocks = tuple(q.reshape(b, h, nb, QBLOCK, q.shape[-1]).transpose(2, 0, 1, 3, 4) for q in qs)
    out = lax.map(lambda a: fn(*a), blocks)
    return out.transpose(1, 2, 0, 3, 4).reshape(b, h, s, out.shape[-1])


def dense_attn(q, k, v, scale):
    def blk(qb):
        s = jnp.einsum('bhqd,bhkd->bhqk', qb, k).astype(jnp.float32) * scale
        p = jax.nn.softmax(s, axis=-1).astype(v.dtype)
        return jnp.einsum('bhqk,bhkd->bhqd', p, v)
    return _map_query_blocks(blk, q)


def diff_attn(q1, q2, k1, k2, v, lam, scale):
    def blk(q1b, q2b):
        s1 = jnp.einsum('bhqd,bhkd->bhqk', q1b, k1).astype(jnp.float32) * scale
        s2 = jnp.einsum('bhqd,bhkd->bhqk', q2b, k2).astype(jnp.float32) * scale
        p = jax.nn.softmax(s1, axis=-1) - lam * jax.nn.softmax(s2, axis=-1)
        return jnp.einsum('bhqk,bhkd->bhqd', p.astype(v.dtype), v)
    return _map_query_blocks(blk, q1, q2)


def mla_attn(q_nope, q_pe, k_nope, k_pe, v, scale):
    def blk(qn, qp):
        s = (jnp.einsum('bhqd,bhkd->bhqk', qn, k_nope)
             + jnp.einsum('bhqr,bkr->bhqk', qp, k_pe)).astype(jnp.float32) * scale
        p = jax.nn.softmax(s, axis=-1).astype(v.dtype)
        return jnp.einsum('bhqk,bhkd->bhqd', p, v)
    return _map_query_blocks(blk, q_nope, q_pe)


def neighbourhood_attn(q, k, v, k_ctx, v_ctx, rpb):
    b, h, s, d = q.shape
    rows = s // GRID_W
    kh = min(NA_WIN_ROWS, rows)
    kw = NA_WIN_COLS
    scale = d ** -0.5
    lc = k_ctx.shape[2]
    cols = jnp.arange(GRID_W)
    c0 = jnp.clip(cols - kw // 2, 0, GRID_W - kw)
    in_win = (cols[None, :] >= c0[:, None]) & (cols[None, :] < c0[:, None] + kw)
    dcol = jnp.clip(cols[None, :] - cols[:, None], -(kw - 1), kw - 1) + (kw - 1)
    qg = q.reshape(b, h, rows, GRID_W, d).transpose(2, 0, 1, 3, 4)
    kg = k.reshape(b, h, rows, GRID_W, d)
    vg = v.reshape(b, h, rows, GRID_W, d)

    def row_block(args):
        r, qr = args
        r0 = jnp.clip(r - kh // 2, 0, rows - kh)
        kr = lax.dynamic_slice_in_dim(kg, r0, kh, axis=2)
        vr = lax.dynamic_slice_in_dim(vg, r0, kh, axis=2)
        drow = r0 + jnp.arange(kh) - r + (NA_WIN_ROWS - 1)
        bias = rpb[:, drow[None, :, None], dcol[:, None, :]]
        s_loc = (jnp.einsum('bhqd,bhikd->bhqik', qr, kr).astype(jnp.float32) * scale
                 + bias.astype(jnp.float32))
        s_loc = jnp.where(in_win[:, None, :], s_loc, NEG_INF)
        s_ctx = jnp.einsum('bhqd,bhkd->bhqk', qr, k_ctx).astype(jnp.float32) * scale
        s_all = jnp.concatenate([s_ctx, s_loc.reshape(b, h, GRID_W, kh * GRID_W)], axis=-1)
        p = jax.nn.softmax(s_all, axis=-1).astype(v.dtype)
        p_loc = p[..., lc:].reshape(b, h, GRID_W, kh, GRID_W)
        return (jnp.einsum('bhqk,bhkd->bhqd', p[..., :lc], v_ctx)
                + jnp.einsum('bhqik,bhikd->bhqd', p_loc, vr))

    out = lax.map(row_block, (jnp.arange(rows), qg))
    return out.transpose(1, 2, 0, 3, 4).reshape(b, h, s, d)


def spatial_gating(u, v, lp):
    u = jax.nn.gelu(u)
    v = jax.nn.gelu(v)
    bsz, length, _ = u.shape
    n = length // SGU_CHUNK
    vg = rmsnorm(v.reshape(bsz, n, SGU_CHUNK, SGU_GROUPS, SGU_GROUP_DIM), lp['sgu_g'])
    mixed = jnp.einsum('gpq,bnqgc->bnpgc', lp['sgu_w'], vg) + lp['sgu_b'].T[:, :, None]
    return u * mixed.reshape(bsz, length, WIDTH_D)


def _in_proj(h, w):
    points = np.cumsum(IN_SPLITS)[:-1].tolist()
    return jnp.split(h @ w, points, axis=-1)


def _diff_lambda(lp, lam_init):
    f = jnp.float32
    return (jnp.exp(jnp.sum((lp['diff_lq1'] * lp['diff_lk1']).astype(f)))
            - jnp.exp(jnp.sum((lp['diff_lq2'] * lp['diff_lk2']).astype(f))) + lam_init)


def _diff_mixer(q, k, v, lp, lam_init):
    lam = _diff_lambda(lp, lam_init)
    o = diff_attn(q[..., :DIFF_QK_DIM], q[..., DIFF_QK_DIM:], k[..., :DIFF_QK_DIM], k[..., DIFF_QK_DIM:],
                  v, lam, DIFF_QK_DIM ** -0.5)
    return rmsnorm(o, lp['diff_g_subln']) * (1.0 - lam_init)


def _rope_pair(x, rows, cols):
    return jnp.concatenate([rope2d(x[..., :DIFF_QK_DIM], rows, cols),
                            rope2d(x[..., DIFF_QK_DIM:], rows, cols)], axis=-1)


def _mla_mixer(q_nope, q_pe, ckv_all, kpe_all, lp):
    k_nope = _heads(ckv_all @ lp['mla_w_uk'], MLA_HEADS)
    v = _heads(ckv_all @ lp['mla_w_uv'], MLA_HEADS)
    return mla_attn(q_nope, q_pe, k_nope, kpe_all, v, (MLA_NOPE + MLA_ROPE) ** -0.5)


def _mix_context(h, lp, lam_init):
    qa, ka, va, qb, kb, vb, qc, ckv, kpe, u, vs = _in_proj(h, lp['w_in'])
    qa, ka, va = _heads(qa, NA_HEADS), _heads(ka, NA_HEADS), _heads(va, NA_HEADS)
    o_a = dense_attn(qa, ka, va, HEAD_DIM ** -0.5)
    qb, kb, vb = _heads(qb, DIFF_HEADS), _heads(kb, DIFF_HEADS), _heads(vb, DIFF_HEADS)
    o_b = _diff_mixer(qb, kb, vb, lp, lam_init)
    ckv = rmsnorm(ckv, lp['mla_g_ckv'])
    qc = _heads(qc, MLA_HEADS)
    o_c = _mla_mixer(qc[..., :MLA_NOPE], qc[..., MLA_NOPE:], ckv, kpe, lp)
    o_d = spatial_gating(u, vs, lp)
    out = jnp.concatenate([_merge(o_a), _merge(o_b), _merge(o_c), o_d], axis=-1)
    return out, (ka, va, kb, vb, ckv, kpe)


def _mix_latent(h, lp, lam_init, na_k, na_v, diff_k, diff_v, mla_ckv, mla_kpe):
    qa, ka, va, qb, kb, vb, qc, ckv, kpe, u, vs = _in_proj(h, lp['w_in'])
    rows, cols = _grid_positions(h.shape[1])
    qa, ka, va = _heads(qa, NA_HEADS), _heads(ka, NA_HEADS), _heads(va, NA_HEADS)
    o_a = neighbourhood_attn(qa, ka, va, na_k, na_v, lp['na_rpb'])
    qb = _rope_pair(_heads(qb, DIFF_HEADS), rows, cols)
    kb = _rope_pair(_heads(kb, DIFF_HEADS), rows, cols)
    vb = _heads(vb, DIFF_HEADS)
    o_b = _diff_mixer(qb, jnp.concatenate([diff_k, kb], axis=2), jnp.concatenate([diff_v, vb], axis=2),
                      lp, lam_init)
    ckv = rmsnorm(ckv, lp['mla_g_ckv'])
    kpe = rope2d(kpe, rows, cols)
    qc = _heads(qc, MLA_HEADS)
    o_c = _mla_mixer(qc[..., :MLA_NOPE], rope2d(qc[..., MLA_NOPE:], rows, cols),
                     jnp.concatenate([mla_ckv, ckv], axis=1), jnp.concatenate([mla_kpe, kpe], axis=1), lp)
    o_d = spatial_gating(u, vs, lp)
    return jnp.concatenate([_merge(o_a), _merge(o_b), _merge(o_c), o_d], axis=-1)


def _block(x, m, lp, mix):
    e = jax.nn.silu(m) @ lp['w_ada'] + lp['b_ada']
    sh1, sc1, g1, sh2, sc2, g2 = jnp.split(e, 6, axis=-1)
    h = rmsnorm(x, lp['g_mix']) * (1 + sc1[:, None]) + sh1[:, None]
    o, extra = mix(h)
    x = x + g1[:, None] * (o @ lp['w_out'])
    hf = rmsnorm(x, lp['g_ffn']) * (1 + sc2[:, None]) + sh2[:, None]
    x = x + g2[:, None] * (jnp.square(jax.nn.relu(hf @ lp['w_ff1'])) @ lp['w_ff2'])
    return x, extra


def setup_inputs(seed: int = 0) -> dict:
    key = jax.random.key(seed)
    ks = jax.random.split(key, 32)
    f32 = jnp.float32

    def nrm(k, shape, s):
        return jax.random.normal(k, shape, f32) * s

    def gain(k, shape):
        return 1.0 + 0.01 * jax.random.normal(k, shape, f32)

    L, D = DEPTH, D_MODEL
    return {
        'x_prompt': nrm(ks[0], (BATCH, SEQ, D), 1.0),
        'x_sample': nrm(ks[1], (DEC_BATCH, DEC_SEQ, D), 1.0),
        'cache_na_k': nrm(ks[2], (DEC_BATCH, L, NA_HEADS, PAST_LEN, HEAD_DIM), 1.0),
        'cache_na_v': nrm(ks[3], (DEC_BATCH, L, NA_HEADS, PAST_LEN, HEAD_DIM), 1.0),
        'cache_diff_k': nrm(ks[4], (DEC_BATCH, L, DIFF_HEADS, PAST_LEN, 2 * DIFF_QK_DIM), 1.0),
        'cache_diff_v': nrm(ks[5], (DEC_BATCH, L, DIFF_HEADS, PAST_LEN, DIFF_V_DIM), 1.0),
        'cache_mla_ckv': nrm(ks[6], (DEC_BATCH, L, PAST_LEN, MLA_KV_RANK), 1.0),
        'cache_mla_kpe': nrm(ks[7], (DEC_BATCH, L, PAST_LEN, MLA_ROPE), 1.0),
        'c': nrm(ks[8], (DEC_BATCH, D), 1.0),
        'c_ctx': nrm(ks[9], (D,), 1.0),
        'w_ada': nrm(ks[10], (L, D, 6 * D), D ** -0.5),
        'b_ada': nrm(ks[11], (L, 6 * D), 0.01),
        'g_mix': gain(ks[12], (L, D)),
        'g_ffn': gain(ks[13], (L, D)),
        'w_in': nrm(ks[14], (L, D, IN_COLS), D ** -0.5),
        'w_out': nrm(ks[15], (L, MIX_WIDTH, D), MIX_WIDTH ** -0.5),
        'na_rpb': nrm(ks[16], (L, NA_HEADS, 2 * NA_WIN_ROWS - 1, 2 * NA_WIN_COLS - 1), 0.1),
        'diff_lq1': nrm(ks[17], (L, DIFF_QK_DIM), 0.1),
        'diff_lk1': nrm(ks[18], (L, DIFF_QK_DIM), 0.1),
        'diff_lq2': nrm(ks[19], (L, DIFF_QK_DIM), 0.1),
        'diff_lk2': nrm(ks[20], (L, DIFF_QK_DIM), 0.1),
        'diff_g_subln': gain(ks[21], (L, DIFF_V_DIM)),
        'mla_g_ckv': gain(ks[22], (L, MLA_KV_RANK)),
        'mla_w_uk': nrm(ks[23], (L, MLA_KV_RANK, MLA_HEADS * MLA_NOPE), MLA_KV_RANK ** -0.5),
        'mla_w_uv': nrm(ks[24], (L, MLA_KV_RANK, MLA_HEADS * MLA_V), MLA_KV_RANK ** -0.5),
        'sgu_g': gain(ks[25], (L, SGU_GROUPS, SGU_GROUP_DIM)),
        'sgu_w': nrm(ks[26], (L, SGU_GROUPS, SGU_CHUNK, SGU_CHUNK), SGU_CHUNK ** -0.5),
        'sgu_b': gain(ks[27], (L, SGU_GROUPS, SGU_CHUNK)),
        'w_ff1': nrm(ks[28], (L, D, D_FF), D ** -0.5),
        'w_ff2': nrm(ks[29], (L, D_FF, D), D_FF ** -0.5),
        'g_final': gain(ks[30], (D,)),
    }


def reference(x_prompt, x_sample, cache_na_k, cache_na_v, cache_diff_k, cache_diff_v,
              cache_mla_ckv, cache_mla_kpe, c, c_ctx,
              w_ada, b_ada, g_mix, g_ffn, w_in, w_out, na_rpb,
              diff_lq1, diff_lk1, diff_lq2, diff_lk2, diff_g_subln,
              mla_g_ckv, mla_w_uk, mla_w_uv, sgu_g, sgu_w, sgu_b, w_ff1, w_ff2, g_final):
    stacked = {
        'w_ada': w_ada, 'b_ada': b_ada, 'g_mix': g_mix, 'g_ffn': g_ffn, 'w_in': w_in, 'w_out': w_out,
        'na_rpb': na_rpb, 'diff_lq1': diff_lq1, 'diff_lk1': diff_lk1, 'diff_lq2': diff_lq2,
        'diff_lk2': diff_lk2, 'diff_g_subln': diff_g_subln, 'mla_g_ckv': mla_g_ckv,
        'mla_w_uk': mla_w_uk, 'mla_w_uv': mla_w_uv, 'sgu_g': sgu_g, 'sgu_w': sgu_w, 'sgu_b': sgu_b,
        'w_ff1': w_ff1, 'w_ff2': w_ff2,
    }
    m_ctx = c_ctx[None, :]
    y_p = x_prompt
    y_s = x_sample
    st = ([], [], [], [], [], [])
    for l in range(DEPTH):
        lp = {name: arr[l] for name, arr in stacked.items()}
        lam_init = 0.8 - 0.6 * math.exp(-0.3 * l)
        y_p, ctx = _block(y_p, m_ctx, lp, lambda h: _mix_context(h, lp, lam_init))
        for lst, t in zip(st, ctx):
            lst.append(t)
        y_s, _ = _block(y_s, c, lp, lambda h: (_mix_latent(
            h, lp, lam_init, cache_na_k[:, l], cache_na_v[:, l], cache_diff_k[:, l], cache_diff_v[:, l],
            cache_mla_ckv[:, l], cache_mla_kpe[:, l]), None))
    y_prompt = rmsnorm(y_p, g_final)
    y_sample = rmsnorm(y_s, g_final)
    new_na_k = jnp.stack(st[0], axis=1)
    new_na_v = jnp.stack(st[1], axis=1)
    new_diff_k = jnp.stack(st[2], axis=1)
    new_diff_v = jnp.stack(st[3], axis=1)
    new_mla_ckv = jnp.stack(st[4], axis=1)
    new_mla_kpe = jnp.stack(st[5], axis=1)
    return (y_prompt, y_sample, new_na_k, new_na_v, new_diff_k, new_diff_v, new_mla_ckv, new_mla_kpe)
```

```python
import math
from contextlib import ExitStack
import numpy as np
import concourse.bass as bass
import concourse.mybir as mybir
from concourse.bass_utils import run_bass_kernel_spmd

F32 = mybir.dt.float32
BF16 = mybir.dt.bfloat16
AF = mybir.ActivationFunctionType
ALU = mybir.AluOpType
AX = mybir.AxisListType

L = 4
D = 1024
NP = 512
NS = 256
NT = NP + NS
EPS = 1e-6
NEG = -30000.0
NBLK_L = 25
RING = 4
SAME_SYNC = {"pe": False, "act": True, "dve": True, "pool": False, "sp": False}


def lam_init_of(l):
    return 0.8 - 0.6 * math.exp(-0.3 * l)


class Res:
    __slots__ = ("name", "w", "r", "x")

    def __init__(self, name, x=False):
        self.name = name
        self.w = None
        self.r = {}
        self.x = x


class Buf:
    __slots__ = ("t", "res")

    def __init__(self, t, res):
        self.t = t
        self.res = res


class Sched:
    def __init__(self, nc, es):
        self.nc = nc
        self.es = es
        self.E = {"pe": nc.tensor, "act": nc.scalar, "dve": nc.vector, "pool": nc.gpsimd, "sp": nc.sync}
        self.sem = {k: es.enter_context(nc.semaphore("prog_" + k)) for k in ("pe", "act", "dve")}
        self.cnt = {k: 0 for k in self.E}
        self.known = {k: {} for k in self.E}
        self.dsem = {q: [es.enter_context(nc.semaphore("d_%s%d" % (q, i))) for i in range(n)]
                     for q, n in (("pool", 10), ("sp", 8))}
        self.dval = {}
        self.drr = {"pool": 0, "sp": 0}
        self.ccsem = es.enter_context(nc.semaphore("ccsem"))
        self.ccn = 0
        self.out_toks = []

    def _wait(self, e, toks):
        need = {}
        for (sem, val, eng) in toks:
            if eng == e and not SAME_SYNC[e]:
                continue
            k = id(sem)
            if self.known[e].get(k, 0) >= val:
                continue
            if k not in need or need[k][1] < val:
                need[k] = (sem, val)
        for sem, val in need.values():
            self.E[e].wait_ge(sem, val)
            self.known[e][id(sem)] = val

    @staticmethod
    def _deps(reads, writes):
        toks = []
        for r in reads:
            if r.w is not None:
                toks.append(r.w)
            if r.x:
                toks.extend(r.r.values())
        for w in writes:
            if w.w is not None:
                toks.append(w.w)
            toks.extend(w.r.values())
        return toks

    def op(self, e, fn, reads=(), writes=()):
        self._wait(e, self._deps(reads, writes))
        ins = fn(self.E[e])
        self.cnt[e] += 1
        ins.then_inc(self.sem[e], 1)
        tok = (self.sem[e], self.cnt[e], e)
        for r in reads:
            r.r[e] = tok
        for w in writes:
            w.w = tok
            w.r = {}
        return tok

    def dma(self, q, out, in_, reads=(), writes=(), is_output=False):
        sems = self.dsem[q]
        sem = sems[self.drr[q] % len(sems)]
        self.drr[q] += 1
        v = self.dval.get(id(sem), 0)
        toks = self._deps(reads, writes)
        if v > 0:
            toks.append((sem, v, None))
        self._wait(q, toks)
        self.E[q].dma_start(out=out, in_=in_).then_inc(sem, 16)
        self.dval[id(sem)] = v + 16
        tok = (sem, v + 16, None)
        key = ("d", id(sem))
        for r in reads:
            r.r[key] = tok
        for w in writes:
            w.w = tok
            w.r = {}
        if is_output:
            self.out_toks.append(tok)
        return tok

    def allgather(self, in_ap, out_ap, reads, writes):
        q = "pool"
        self._wait(q, self._deps(reads, writes))
        ins = self.nc.gpsimd.collective_compute("AllGather", ALU.bypass,
                                                replica_groups=[[0, 1, 2, 3], [4, 5, 6, 7]],
                                                ins=[in_ap.opt()], outs=[out_ap.opt()])
        ins.then_inc(self.ccsem)
        self.ccn += 1
        tok = (self.ccsem, self.ccn, None)
        key = ("cc",)
        for r in reads:
            r.r[key] = tok
        for w in writes:
            w.w = tok
            w.r = {}
        return tok

    def finish(self):
        toks = list(self.out_toks)
        for q in self.dsem:
            for sem in self.dsem[q]:
                v = self.dval.get(id(sem), 0)
                if v > 0:
                    toks.append((sem, v, None))
        if self.ccn > 0:
            toks.append((self.ccsem, self.ccn, None))
        for e in ("pe", "act", "dve"):
            if self.cnt[e] > 0:
                toks.append((self.sem[e], self.cnt[e], e))
        self._wait("sp", toks)


class Prog:
    def __init__(self):
        self.nc = bass.Bass("TRN2", target_bir_lowering=False)
        self.es = ExitStack()

    def din(self, name, shape, dt=F32):
        return self.nc.dram_tensor(name, list(shape), dt, kind="ExternalInput").ap()

    def dout(self, name, shape, dt=F32):
        return self.nc.dram_tensor(name, list(shape), dt, kind="ExternalOutput").ap()

    def sb(self, name, shape, dt):
        t = self.es.enter_context(self.nc.sbuf_tensor("sb_" + name, list(shape), dt))
        return Buf(t, Res(name))

    def build(self, stage=99, nlayers=L):
        nc = self.nc
        es = self.es
        self.stage = stage
        self.nblk = L * NBLK_L
        with es:
            self.S = Sched(nc, es)
            self._declare_io()
            self._alloc()
            self._setup()
            if stage >= 1:
                for j in range(3):
                    self._ada_block(0, j)
                self._ada_exchange(0)
                self._ada_finish(0, 0)
                self._ada_finish(0, 1)
            for l in range(nlayers):
                self._layer(l)
            if stage >= 9:
                self._final()
            self.S.finish()
        return nc

    def _declare_io(self):
        d = self.din
        self.i_xT = d("xT", [128, 8 * NT])
        self.i_mT = d("mT", [128, 16])
        self.i_w = d("wblk", [self.nblk, 128, 4096])
        self.i_wa = d("wada", [L * 3, 128, 4096])
        self.i_badap = d("badap", [128, L * 12])
        self.bounce_e = [self.nc.dram_tensor("bouncee%d" % l, [128, 24], F32).ap() for l in range(L)]
        self.gath_e = [self.nc.dram_tensor("gathe%d" % l, [512, 24], F32).ap() for l in range(L)]
        self.bounce_e_res = [Res("bouncee%d" % l) for l in range(L)]
        self.gath_e_res = [Res("gathe%d" % l) for l in range(L)]
        self.blocks = [self.i_wa[j] for j in range(3)]
        for l in range(L):
            self.blocks += [self.i_w[NBLK_L * l + j] for j in range(7)]
            if l + 1 < L:
                self.blocks += [self.i_wa[3 * (l + 1) + j] for j in range(3)]
            self.blocks += [self.i_w[NBLK_L * l + 7 + j] for j in range(18)]
        self.i_bada = d("bada", [128, L * 48])
        self.i_gmix = d("gmix", [128, L * 8])
        self.i_gffn = d("gffn", [128, L * 8])
        self.i_lqk = d("lqk", [128, L * 4 * 32])
        self.i_gsub = d("gsub", [128, L])
        self.i_gckvf = d("gckvf", [128, L])
        self.i_gckvb = d("gckvb", [128, L * 128])
        self.i_wukT = d("wukT", [128, L * 2 * 128])
        self.i_wuv = d("wuv", [128, L * 256])
        self.i_sgug = d("sgug", [128, L * 256])
        self.i_sguw = d("sguw", [128, L * 4 * 128])
        self.i_sgub = d("sgub", [128, L * 2 * 128])
        self.i_gfin = d("gfin", [128, 1024])
        self.i_ident = d("ident", [128, 128])
        self.i_ropeC = d("ropeC", [128, 256])
        self.i_ropeS = d("ropeS", [128, 256])
        self.i_gbias = d("gbias", [L * 4, 128, 8 * 256])
        self.i_ckaT = d("ckaT", [L, 128, 2 * 512])
        self.i_ckbT = d("ckbT", [L, 128, 2 * 512])
        self.i_cva = d("cva", [L, 128, 4 * 256])
        self.i_cvb = d("cvb", [L, 128, 4 * 256])
        self.i_cckvT = d("cckvT", [L, 128, 512])
        self.i_ckpe4 = d("ckpe4", [L, 128, 512])
        o = self.dout
        self.o_yp = o("yp", [NP, D])
        self.o_ys = o("ys", [NS, D])
        self.o_nak = o("onak", [2, L, 4, 256, 64])
        self.o_nav = o("onav", [2, L, 4, 256, 64])
        self.o_dk = o("odk", [2, L, 4, 256, 64])
        self.o_dv = o("odv", [2, L, 4, 256, 64])
        self.o_ckv = o("ockv", [2, L, 256, 128])
        self.o_kpe = o("okpe", [2, L, 256, 32])
        self.bounce = [self.nc.dram_tensor("bounce%d" % l, [128, 2048], BF16).ap() for l in range(L)]
        self.gath = [self.nc.dram_tensor("gath%d" % l, [512, 2048], BF16).ap() for l in range(L)]
        self.bounce2 = [self.nc.dram_tensor("bounceb%d" % l, [128, 512], BF16).ap() for l in range(L)]
        self.gath2 = [self.nc.dram_tensor("gathb%d" % l, [512, 512], BF16).ap() for l in range(L)]
        self.bounce2_res = [Res("bounceb%d" % l) for l in range(L)]
        self.gath2_res = [Res("gathb%d" % l) for l in range(L)]
        self.bounce_res = [Res("bounce%d" % l) for l in range(L)]
        self.gath_res = [Res("gath%d" % l) for l in range(L)]

    def _alloc(self):
        sb = self.sb
        nc = self.nc
        self.xT = sb("xT_sb", [128, 8, NT], F32)
        self.hT = sb("hT_sb", [128, 8, NT], BF16)
        self.xres = [[Res("x%d_%d" % (g, k)) for k in range(8)] for g in range(2)]
        self.hres = [[Res("h%d_%d" % (g, k)) for k in range(8)] for g in range(2)]
        self.ring = [sb("ring%d" % i, [128, 4096], BF16) for i in range(RING)]
        self.ring_issued = 0
        self.ring_used = 0
        self.banks = []
        for i in range(8):
            t = self.es.enter_context(nc.psum_tensor("ps%d" % i, [128, 512], F32))
            self.banks.append(Buf(t, Res("ps%d" % i, x=True)))
        self.gen_rr = 0
        self.acc_rr = 0
        self.ident = sb("ident", [128, 128], F32)
        self.identb = sb("identb", [128, 128], BF16)
        self.ones = sb("ones", [128, 128], BF16)
        self.bdones = sb("bdones", [128, 128], BF16)
        self.epsc = sb("epsc", [128, 1], F32)
        self.mT = sb("mT", [128, 16], F32)
        self.sT = sb("sT", [128, 16], BF16)
        self.bada = sb("bada", [128, L, 48], F32)
        self.badap = sb("badap", [128, L, 12], F32)
        self.epart = sb("epart", [128, L, 24], F32)
        self.gmix = sb("gmix", [128, L, 8], F32)
        self.gffn = sb("gffn", [128, L, 8], F32)
        self.lqk = sb("lqk", [128, L, 4, 32], F32)
        self.lqp = sb("lqp", [128, L, 2, 32], F32)
        self.lqe = sb("lqe", [128, L, 2], F32)
        self.nlam = sb("nlam", [128, L], F32)
        self.gsub = sb("gsub", [128, L], F32)
        self.gckvf = sb("gckvf", [128, L], F32)
        self.gckvb = sb("gckvb", [128, L, 128], F32)
        self.wukT = sb("wukT", [128, L, 2, 128], BF16)
        self.wuv = sb("wuv", [128, L, 256], BF16)
        self.sgug = sb("sgug", [128, L, 256], F32)
        self.sguw = sb("sguw", [128, L, 4, 128], BF16)
        self.sgub = sb("sgub", [128, L, 2, 128], F32)
        self.ropeC = sb("ropeC", [128, 256], F32)
        self.ropeS = sb("ropeS", [128, 256], F32)
        self.esb = [sb("esb%d" % l, [128, 48, 2], F32) for l in range(L)]
        self.mod = [sb("mod%d" % l, [128, 2, 6, 8], F32) for l in range(L)]
        self.mod_res = [[Res("mod%d_%d" % (l, p)) for p in range(2)] for l in range(L)]
        self.qaT = [sb("qaT_p", [128, 2, NP], BF16), sb("qaT_s", [128, 2, NS], BF16)]
        self.qbT = [sb("qbT_p", [128, 2, NP], BF16), sb("qbT_s", [128, 2, NS], BF16)]
        self.qnT = [sb("qnT_p", [128, 2, NP], BF16), sb("qnT_s", [128, 2, NS], BF16)]
        self.qaT_res = [[Res("qaT%d_%d" % (g, j)) for j in range(2)] for g in range(2)]
        self.qbT_res = [[Res("qbT%d_%d" % (g, j)) for j in range(2)] for g in range(2)]
        self.qnT_res = [[Res("qnT%d_%d" % (g, j)) for j in range(2)] for g in range(2)]
        self.kaT_res = [Res("kaT_p%d" % j) for j in range(2)]
        self.kbT_res = [Res("kbT_p%d" % j) for j in range(2)]
        self.qpeT = [sb("qpeT_p", [128, NP], BF16), sb("qpeT_s", [128, NS], BF16)]
        self.qabsT = [sb("qabsT_p", [128, 4, NP], BF16), sb("qabsT_s", [128, 4, NS], BF16)]
        self.uT = [sb("uT_p", [128, 2, NP], BF16), sb("uT_s", [128, 2, NS], BF16)]
        self.vg = [sb("vg_p", [128, 4, 256], BF16), sb("vg_s", [128, 2, 256], BF16)]
        self.kaT_p = sb("kaT_p", [128, 2, NP], BF16)
        self.kbT_p = sb("kbT_p", [128, 2, NP], BF16)
        self.ckvT_p = sb("ckvT_p", [128, NP], BF16)
        self.kpe4_p = sb("kpe4_p", [128, NP], BF16)
        self.Vp = sb("Vp", [128, 4, 3, 4, 128], BF16)
        self.Vp_res = [[Res("Vp%d_%d" % (t, m)) for m in range(3)] for t in range(4)]
        self.xb = sb("xb", [128, 2560], BF16)
        self.xb_res = {k: Res("xb_" + k) for k in ("ka0", "ka1", "kb0", "kb1", "ckv", "kpe", "va0", "va1", "vb0", "vb1")}
        self.kvk = [sb("kvk%d" % i, [128, 2, 1536], BF16) for i in range(2)]
        self.kvvf = sb("kvvf", [128, 12 * 512], BF16)
        self.kvk_res = [[Res("kvk%d_%d" % (i, k)) for k in range(5)] for i in range(2)]
        self.uF_res = [[Res("uF%d_%d" % (t, fc)) for fc in range(4)] for t in range(2)]
        self.kvv_res = [Res("kvv_%d" % i) for i in range(12)]
        self.G = [sb("G%d" % i, [128, 8, 256], BF16) for i in range(2)]
        self.g_rr = 0
        self.E = [sb("E%d" % i, [128, 512], BF16) for i in range(4)]
        self.e_rr = 0
        self.f32t = [sb("f32t%d" % i, [128, 512], F32) for i in range(6)]
        self.f_rr = 0
        self.rstd_t = [sb("rstdt%d" % i, [128, 512], F32) for i in range(2)]
        self.r_rr = 0
        self.opair_t = [sb("opair%d" % i, [128, 512], F32) for i in range(2)]
        self.o_rr = 0
        self.b16t = [sb("b16t%d" % i, [128, 512], BF16) for i in range(2)]
        self.b_rr = 0
        self.stg = [sb("stg%d" % i, [128, 512], F32) for i in range(3)]
        self.s_rr = 0
        self.small = [sb("small%d" % i, [128, 16], F32) for i in range(6)]
        self.sm_rr = 0
        self.evac_rr = 0

    def kvv_h(self, i):
        return self.kvvf.t[:, 512 * i:512 * i + 512].rearrange("p (h c) -> p h c", h=4)

    def kvv_lhsT(self, i, h):
        return self.kvvf.t[:, 512 * i + 128 * h:512 * i + 128 * h + 128]

    def uF(self, fc, a, b):
        o = NT * (fc % 2)
        return self.kvk[0].t[:, fc // 2, o + a:o + b]

    def bank(self):
        b = self.banks[self.gen_rr % 4]
        self.gen_rr += 1
        return b

    def abank(self):
        b = self.banks[4 + self.acc_rr % 3]
        self.acc_rr += 1
        return b

    def ft(self):
        b = self.f32t[self.f_rr % len(self.f32t)]
        self.f_rr += 1
        return b

    def bt(self):
        b = self.b16t[self.b_rr % len(self.b16t)]
        self.b_rr += 1
        return b

    def et(self):
        b = self.E[self.e_rr % len(self.E)]
        self.e_rr += 1
        return b

    def st(self):
        b = self.stg[self.s_rr % len(self.stg)]
        self.s_rr += 1
        return b

    def smt(self):
        b = self.small[self.sm_rr % len(self.small)]
        self.sm_rr += 1
        return b

    def act(self, out, in_, func, reads, writes, scale=1.0, bias=None, accum=None):
        kw = {}
        if bias is not None:
            kw["bias"] = bias
        if accum is not None:
            kw["accum_out"] = accum
        return self.S.op("act", lambda e: e.activation(out=out, in_=in_, func=func, scale=scale, **kw), reads, writes)

    def tt(self, out, in0, in1, op, reads, writes):
        return self.S.op("dve", lambda e: e.tensor_tensor(out=out, in0=in0, in1=in1, op=op), reads, writes)

    def ts(self, out, in0, s1, op0, reads, writes, s2=None, op1=None):
        if op1 is None:
            return self.S.op("dve", lambda e: e.tensor_scalar(out=out, in0=in0, scalar1=s1, scalar2=None, op0=op0), reads, writes)
        return self.S.op("dve", lambda e: e.tensor_scalar(out=out, in0=in0, scalar1=s1, scalar2=s2, op0=op0, op1=op1), reads, writes)

    def stt(self, out, in0, scalar, in1, op0, op1, reads, writes):
        return self.S.op("dve", lambda e: e.scalar_tensor_tensor(out=out, in0=in0, scalar=scalar, in1=in1, op0=op0, op1=op1), reads, writes)

    def cp(self, out, in_, reads, writes, eng=None, scale=None):
        if eng is None:
            eng = "dve" if (self.evac_rr % 3 == 0) else "act"
            self.evac_rr += 1
        if eng == "act":
            return self.act(out, in_, AF.Copy, reads, writes, scale=(1.0 if scale is None else scale))
        if scale is None:
            return self.S.op("dve", lambda e: e.tensor_copy(out=out, in_=in_), reads, writes)
        return self.ts(out, in_, scale, ALU.mult, reads, writes)

    def pe(self, fn, reads, writes):
        return self.S.op("pe", fn, reads, writes)

    def ring_issue(self):
        i = self.ring_issued
        if i >= len(self.blocks):
            return
        slot = self.ring[i % RING]
        self.S.dma("pool", slot.t[:, :], self.blocks[i], reads=(), writes=[slot.res])
        self.ring_issued += 1

    def ring_acquire(self):
        i = self.ring_used
        assert i < self.ring_issued
        return self.ring[i % RING]

    def ring_release(self):
        self.ring_used += 1
        self.ring_issue()

    def _setup(self):
        S = self.S
        for _ in range(RING):
            self.ring_issue()

        def ld(q, buf, src):
            S.dma(q, buf.t[:], src, reads=(), writes=[buf.res])

        def v2(ap, pat, **kw):
            return ap.rearrange(pat, **kw)

        ld("sp", self.mT, self.i_mT)
        S.dma("sp", self.bada.t[:], v2(self.i_bada, "p (l c) -> p l c", l=L), writes=[self.bada.res])
        S.dma("sp", self.badap.t[:], v2(self.i_badap, "p (l c) -> p l c", l=L), writes=[self.badap.res])
        S.dma("sp", self.gmix.t[:], v2(self.i_gmix, "p (l c) -> p l c", l=L), writes=[self.gmix.res])
        S.dma("sp", self.gffn.t[:], v2(self.i_gffn, "p (l c) -> p l c", l=L), writes=[self.gffn.res])
        for g, (c0, n) in enumerate(((0, NP), (NP, NS))):
            src = self.i_xT.rearrange("p (k t) -> p k t", k=8)
            S.dma("sp", self.xT.t[:, :, c0:c0 + n], src[:, :, c0:c0 + n], writes=self.xres[g])
        ld("sp", self.ident, self.i_ident)
        S.dma("pool", self.identb.t[:], self.i_ident, writes=[self.identb.res])
        S.dma("sp", self.lqk.t[:], v2(self.i_lqk, "p (l a c) -> p l a c", l=L, a=4), writes=[self.lqk.res])
        ld("sp", self.gsub, self.i_gsub)
        ld("sp", self.gckvf, self.i_gckvf)
        S.dma("sp", self.gckvb.t[:], v2(self.i_gckvb, "p (l c) -> p l c", l=L), writes=[self.gckvb.res])
        S.dma("pool", self.wukT.t[:], v2(self.i_wukT, "p (l a c) -> p l a c", l=L, a=2), writes=[self.wukT.res])
        S.dma("pool", self.wuv.t[:], v2(self.i_wuv, "p (l c) -> p l c", l=L), writes=[self.wuv.res])
        S.dma("sp", self.sgug.t[:], v2(self.i_sgug, "p (l c) -> p l c", l=L), writes=[self.sgug.res])
        S.dma("pool", self.sguw.t[:], v2(self.i_sguw, "p (l a c) -> p l a c", l=L, a=4), writes=[self.sguw.res])
        S.dma("sp", self.sgub.t[:], v2(self.i_sgub, "p (l a c) -> p l a c", l=L, a=2), writes=[self.sgub.res])
        ld("sp", self.ropeC, self.i_ropeC)
        ld("sp", self.ropeS, self.i_ropeS)
        S.op("dve", lambda e: e.memset(self.ones.t[:], 1.0), (), [self.ones.res])
        S.op("dve", lambda e: e.memset(self.bdones.t[:], 0.0), (), [self.bdones.res])
        S.op("dve", lambda e: e.memset(self.bdones.t[0:64, 0:64], 1.0), (), [self.bdones.res])
        S.op("dve", lambda e: e.memset(self.bdones.t[64:128, 64:128], 1.0), (), [self.bdones.res])
        S.op("dve", lambda e: e.memset(self.epsc.t[:], EPS), (), [self.epsc.res])
        for t in range(4):
            for m in range(3):
                S.op("dve", lambda e, t=t, m=m: e.memset(self.Vp.t[:, t, m, :, 64:128], 1.0), (), [self.Vp_res[t][m]])
        for i in range(12):
            S.op("dve", lambda e, i=i: e.memset(self.kvv_h(i)[:, :, 64:128], 1.0), (), [self.kvv_res[i]])
        self.act(self.sT.t[:], self.mT.t[:], AF.Silu, [self.mT.res], [self.sT.res])
        self.tt(self.lqp.t[:], self.lqk.t[:, :, 0:2, :], self.lqk.t[:, :, 2:4, :], ALU.mult, [self.lqk.res], [self.lqp.res])
        S.op("dve", lambda e: e.tensor_reduce(out=self.lqe.t[:], in_=self.lqp.t[:], axis=AX.X, op=ALU.add),
             [self.lqp.res], [self.lqe.res])
        self.act(self.lqe.t[:], self.lqe.t[:], AF.Exp, [self.lqe.res], [self.lqe.res])
        for l in range(L):
            self.ts(self.nlam.t[:, l:l + 1], self.lqe.t[:, l, 1:2], self.lqe.t[:, l, 0:1], ALU.subtract,
                    [self.lqe.res], [self.nlam.res], s2=-lam_init_of(l), op1=ALU.add)
            self.ts(self.gsub.t[:, l:l + 1], self.gsub.t[:, l:l + 1], 1.0 - lam_init_of(l), ALU.mult,
                    [self.gsub.res], [self.gsub.res])

    def _ada_block(self, l, j):
        w = self.ring_acquire()
        ps = self.banks[7]
        cb = 24 * (l % 2)

        def fn(e):
            ins = None
            for cc in range(4):
                ch = 4 * j + cc
                for k in range(8):
                    ins = e.matmul(ps.t[:, cb + 2 * ch:cb + 2 * ch + 2], lhsT=w.t[:, 512 * k + 128 * cc:512 * k + 128 * cc + 128],
                                   rhs=self.sT.t[:, 2 * k:2 * k + 2], start=(k == 0), stop=(k == 7))
            return ins
        self.pe(fn, [w.res, self.sT.res], [ps.res])
        self.ring_release()

    def _ada_exchange(self, l):
        S = self.S
        ps = self.banks[7]
        cb = 24 * (l % 2)
        ep = self.epart
        for v in range(2):
            self.tt(ep.t[:, l, :].rearrange("p (c v) -> p c v", v=2)[:, :, v],
                    ps.t[:, cb:cb + 24].rearrange("p (c v) -> p c v", v=2)[:, :, v], self.badap.t[:, l, :], ALU.add,
                    [ps.res, self.badap.res], [ep.res])
        S.dma("pool", self.bounce_e[l], ep.t[:, l, :], reads=[ep.res], writes=[self.bounce_e_res[l]])
        S.allgather(self.bounce_e[l], self.gath_e[l], reads=[self.bounce_e_res[l]], writes=[self.gath_e_res[l]])
        esb = self.esb[l]
        for r in range(4):
            S.dma("sp", esb.t[:, 12 * r:12 * r + 12, :], self.gath_e[l][128 * r:128 * r + 128, :].rearrange("p (c v) -> p c v", v=2),
                  reads=[self.gath_e_res[l]], writes=[esb.res])

    def _ada_finish(self, l, part):
        esb = self.esb[l]
        mod = self.mod[l]
        mres = self.mod_res[l][part]
        c0 = 24 * part
        gvec = self.gmix if part == 0 else self.gffn
        for v in range(2):
            self.cp(mod.t[:, v, 3 * part + 0, :], esb.t[:, c0:c0 + 8, v], [esb.res], [mres], eng="dve")
            self.stt(mod.t[:, v, 3 * part + 1, :], esb.t[:, c0 + 8:c0 + 16, v], 1.0, gvec.t[:, l, :], ALU.add, ALU.mult,
                     [esb.res, gvec.res], [mres])
            self.cp(mod.t[:, v, 3 * part + 2, :], esb.t[:, c0 + 16:c0 + 24, v], [esb.res], [mres], eng="dve")

    GR = ((0, NP, 0), (NP, NS, 1))

    def _rstd_from_ps(self, ps, n, inv_count, rows=slice(0, 128)):
        r = self.rstd_t[self.r_rr % 2]
        self.r_rr += 1
        self.act(r.t[rows, :n], ps.t[rows, :n], AF.Ln, [ps.res, self.epsc.res], [r.res], scale=inv_count, bias=self.epsc.t[rows, :])
        self.act(r.t[rows, :n], r.t[rows, :n], AF.Exp, [r.res], [r.res], scale=-0.5)
        return r

    def _norm(self, l, g, which):
        c0, n, v = self.GR[g]
        sh_i, gsc_i = (0, 1) if which == 1 else (3, 4)
        mod = self.mod[l]
        mres = self.mod_res[l][0 if which == 1 else 1]
        ssb = self.bank()
        for k in range(8):
            sq = self.bt()
            self.act(sq.t[:, :n], self.xT.t[:, k, c0:c0 + n], AF.Square, [self.xres[g][k]], [sq.res])
            self.pe(lambda e, k=k, sq=sq: e.matmul(ssb.t[:, :n], lhsT=self.ones.t[:, :], rhs=sq.t[:, :n], start=(k == 0), stop=(k == 7)),
                    [sq.res, self.ones.res], [ssb.res])
        import os
        dn = int(os.environ.get("DBG_NORM", "9"))
        if dn < 2:
            return
        rstd = self._rstd_from_ps(ssb, n, 1.0 / D)
        if dn < 3:
            return
        for k in range(8):
            t = self.ft()
            self.stt(t.t[:, :n], self.xT.t[:, k, c0:c0 + n], mod.t[:, v, gsc_i, k:k + 1], rstd.t[:, :n], ALU.mult, ALU.mult,
                     [self.xres[g][k], mres, rstd.res], [t.res])
            if dn < 4:
                continue
            self.act(self.hT.t[:, k, c0:c0 + n], t.t[:, :n], AF.Identity, [t.res, mres], [self.hres[g][k]],
                     bias=mod.t[:, v, sh_i, k:k + 1])

    def _fm(self, g, w, col0, M):
        c0, n, _ = self.GR[g]
        ps = self.bank()

        def fn(e):
            ins = None
            for k in range(8):
                ins = e.matmul(ps.t[0:M, :n], lhsT=w.t[:, 512 * k + col0:512 * k + col0 + M], rhs=self.hT.t[:, k, c0:c0 + n],
                               start=(k == 0), stop=(k == 7))
            return ins
        self.pe(fn, [w.res] + self.hres[g], [ps.res])
        return ps

    def _tm(self, g, t, w, col0, N):
        c0, n, _ = self.GR[g]
        ps = self.bank()
        a = c0 + 128 * t

        def fn(e):
            ins = None
            for k in range(8):
                ins = e.matmul(ps.t[:, :N], lhsT=self.hT.t[:, k, a:a + 128], rhs=w.t[:, 512 * k + col0:512 * k + col0 + N],
                               start=(k == 0), stop=(k == 7))
            return ins
        self.pe(fn, [w.res] + self.hres[g], [ps.res])
        return ps

    def _rope(self, out, out_res, psA, psB, n):
        t1 = self.ft()
        t2 = self.ft()
        self.tt(t1.t[:, :n], psA.t[:, :n], self.ropeC.t[:, :n], ALU.mult, [psA.res, self.ropeC.res], [t1.res])
        self.tt(t2.t[:, :n], psB.t[:, :n], self.ropeS.t[:, :n], ALU.mult, [psB.res, self.ropeS.res], [t2.res])
        self.tt(out, t1.t[:, :n], t2.t[:, :n], ALU.add, [t1.res, t2.res], [out_res])

    def _out_dma(self, dst, src_ap, src_res):
        self.S.dma("sp", dst, src_ap, reads=[src_res], writes=(), is_output=True)

    def _hd(self, o, s, l, r0):
        return o[s, l].rearrange("h s d -> s h d")[r0:r0 + 128]

    def _in_proj(self, l):
        S = self.S
        xb = self.xb
        import os
        dbg_b = int(os.environ.get("DBG_B", "9"))
        dbg_g = int(os.environ.get("DBG_G", "2"))
        for b in (6, 0, 1, 2, 3, 4, 5):
            w = self.ring_acquire()
            for g in (1, 0):
                if b > dbg_b or g >= dbg_g:
                    continue
                c0, n, v = self.GR[g]
                isP = (g == 0)
                if b == 0:
                    for j in range(2):
                        ps = self._fm(g, w, 128 * j, 128)
                        self.cp(self.qaT[g].t[:, j, :], ps.t[:, :n], [ps.res], [self.qaT_res[g][j]], scale=0.125)
                    for j in range(2):
                        ps = self._fm(g, w, 256 + 128 * j, 128)
                        if isP:
                            self.cp(self.kaT_p.t[:, j, :], ps.t[:, :n], [ps.res], [self.kaT_res[j]])
                        else:
                            self.cp(xb.t[:, 256 * j:256 * j + 256], ps.t[:, :n], [ps.res], [self.xb_res["ka%d" % j]])
                    if isP:
                        for t in range(4):
                            ps = self._tm(g, t, w, 256, 256)
                            st = self.st()
                            self.cp(st.t[:, 0:256], ps.t[:, 0:256], [ps.res], [st.res])
                            self._out_dma(self._hd(self.o_nak, t // 2, l, 128 * (t % 2)),
                                          st.t[:, 0:256].rearrange("p (h d) -> p h d", h=4), st.res)
                elif b in (1, 2):
                    for j in range(2):
                        psA = self._fm(g, w, 128 * j, 128)
                        if isP:
                            dst = self.qbT[0] if b == 1 else self.kbT_p
                            dres = self.qbT_res[0][j] if b == 1 else self.kbT_res[j]
                            self.cp(dst.t[:, j, :], psA.t[:, :n], [psA.res], [dres])
                        else:
                            psB = self._fm(g, w, 256 + 128 * j, 128)
                            if b == 1:
                                self._rope(self.qbT[1].t[:, j, :], self.qbT_res[1][j], psA, psB, n)
                            else:
                                self._rope(xb.t[:, 512 + 256 * j:512 + 256 * j + 256], self.xb_res["kb%d" % j], psA, psB, n)
                    if isP and b == 2:
                        for t in range(4):
                            ps = self._tm(g, t, w, 0, 256)
                            st = self.st()
                            self.cp(st.t[:, 0:256], ps.t[:, 0:256], [ps.res], [st.res])
                            self._out_dma(self._hd(self.o_dk, t // 2, l, 128 * (t % 2)),
                                          st.t[:, 0:256].rearrange("p (h d) -> p h d", h=4), st.res)
                elif b == 3:
                    dx = int(os.environ.get("DBG_X", "9"))
                    for t in range(n // 128):
                        ps = self._tm(g, t, w, 0, 512)
                        if dx < 2:
                            continue
                        if isP:
                            st = self.st()
                            self.cp(st.t[:, :], ps.t[:, :], [ps.res], [st.res])
                            if dx >= 3:
                                self._out_dma(self._hd(self.o_nav, t // 2, l, 128 * (t % 2)),
                                              st.t[:, 0:256].rearrange("p (h d) -> p h d", h=4), st.res)
                                self._out_dma(self._hd(self.o_dv, t // 2, l, 128 * (t % 2)),
                                              st.t[:, 256:512].rearrange("p (h d) -> p h d", h=4), st.res)
                            if dx >= 4:
                                for m in range(2):
                                    self.cp(self.Vp.t[:, t, m, :, 0:64], ps.t[:, 256 * m:256 * m + 256].rearrange("p (h d) -> p h d", h=4),
                                            [ps.res], [self.Vp_res[t][m]])
                        else:
                            self.cp(xb.t[:, 1024 + 256 * t:1024 + 256 * t + 256], ps.t[:, 0:256], [ps.res], [self.xb_res["va%d" % t]])
                            self.cp(xb.t[:, 1536 + 256 * t:1536 + 256 * t + 256], ps.t[:, 256:512], [ps.res], [self.xb_res["vb%d" % t]])
                    if not isP:
                        r1 = [self.xb_res[k] for k in ("ka0", "ka1", "kb0", "kb1", "va0", "va1", "vb0", "vb1")]
                        S.dma("pool", self.bounce[l], xb.t[:, 0:2048], reads=r1, writes=[self.bounce_res[l]])
                        S.allgather(self.bounce[l], self.gath[l], reads=[self.bounce_res[l]], writes=[self.gath_res[l]])
                elif b == 4:
                    for j in range(2):
                        ps = self._fm(g, w, 128 * j, 128)
                        self.cp(self.qnT[g].t[:, j, :], ps.t[:, :n], [ps.res], [self.qnT_res[g][j]])
                    ps = self._fm(g, w, 256, 128)
                    craw = self.ft()
                    sq = self.bt()
                    self.act(craw.t[:, :n], ps.t[:, :n], AF.Copy, [ps.res], [craw.res])
                    self.act(sq.t[:, :n], ps.t[:, :n], AF.Square, [ps.res], [sq.res])
                    ssb = self.bank()
                    self.pe(lambda e, sq=sq, ssb=ssb, n=n: e.matmul(ssb.t[:, :n], lhsT=self.ones.t[:, :], rhs=sq.t[:, :n], start=True, stop=True),
                            [sq.res, self.ones.res], [ssb.res])
                    rstd = self._rstd_from_ps(ssb, n, 1.0 / 128)
                    if isP:
                        dst_ap, dst_res = self.ckvT_p.t[:, :], self.ckvT_p.res
                    else:
                        dst_ap, dst_res = xb.t[:, 2048:2304], self.xb_res["ckv"]
                    self.stt(dst_ap, craw.t[:, :n], self.gckvf.t[:, l:l + 1], rstd.t[:, :n], ALU.mult, ALU.mult,
                             [craw.res, self.gckvf.res, rstd.res], [dst_res])
                    if isP:
                        for t in range(4):
                            ps = self._tm(g, t, w, 256, 128)
                            junk = self.ft()
                            ss = self.smt()
                            self.act(junk.t[:, 0:128], ps.t[:, 0:128], AF.Square, [ps.res], [junk.res, ss.res], accum=ss.t[:, 0:1])
                            self.act(ss.t[:, 1:2], ss.t[:, 0:1], AF.Ln, [ss.res, self.epsc.res], [ss.res], scale=1.0 / 128, bias=self.epsc.t[:, :])
                            self.act(ss.t[:, 2:3], ss.t[:, 1:2], AF.Exp, [ss.res], [ss.res], scale=-0.5)
                            st = self.st()
                            self.stt(st.t[:, 0:128], ps.t[:, 0:128], ss.t[:, 2:3], self.gckvb.t[:, l, :], ALU.mult, ALU.mult,
                                     [ps.res, ss.res, self.gckvb.res], [st.res])
                            self._out_dma(self.o_ckv[t // 2, l, 128 * (t % 2):128 * (t % 2) + 128, :], st.t[:, 0:128], st.res)
                elif b == 5:
                    if isP:
                        ps = self._fm(g, w, 0, 128)
                        self.cp(self.qpeT[0].t[:, :], ps.t[:, :n], [ps.res], [self.qpeT[0].res])
                        ps = self._fm(g, w, 256, 128)
                        self.cp(self.kpe4_p.t[:, :], ps.t[:, :n], [ps.res], [self.kpe4_p.res])
                        for t in range(4):
                            ps = self._tm(g, t, w, 256, 32)
                            st = self.st()
                            self.cp(st.t[:, 0:32], ps.t[:, 0:32], [ps.res], [st.res])
                            self._out_dma(self.o_kpe[t // 2, l, 128 * (t % 2):128 * (t % 2) + 128, :], st.t[:, 0:32], st.res)
                    else:
                        psA = self._fm(g, w, 0, 128)
                        psB = self._fm(g, w, 128, 128)
                        self._rope(self.qpeT[1].t[:, :], self.qpeT[1].res, psA, psB, n)
                        psA = self._fm(g, w, 256, 128)
                        psB = self._fm(g, w, 384, 128)
                        self._rope(xb.t[:, 2304:2560], self.xb_res["kpe"], psA, psB, n)
                        S.dma("pool", self.bounce2[l], xb.t[:, 2048:2560], reads=[self.xb_res["ckv"], self.xb_res["kpe"]], writes=[self.bounce2_res[l]])
                        S.allgather(self.bounce2[l], self.gath2[l], reads=[self.bounce2_res[l]], writes=[self.gath2_res[l]])
                else:
                    for j in range(2):
                        ps = self._fm(g, w, 128 * j, 128)
                        self.act(self.uT[g].t[:, j, :], ps.t[:, :n], AF.Gelu_apprx_tanh, [ps.res], [self.uT[g].res])
                    for t in range(n // 128):
                        ps = self._tm(g, t, w, 256, 256)
                        gv = self.ft()
                        sq = self.ft()
                        ss = self.smt()
                        self.act(gv.t[:, 0:256], ps.t[:, 0:256], AF.Gelu_apprx_tanh, [ps.res], [gv.res])
                        self.tt(sq.t[:, 0:256], gv.t[:, 0:256], gv.t[:, 0:256], ALU.mult, [gv.res], [sq.res])
                        S.op("dve", lambda e, ss=ss, sq=sq: e.tensor_reduce(out=ss.t[:, 0:4], in_=sq.t[:, 0:256].rearrange("p (g c) -> p g c", g=4),
                                                                         axis=AX.X, op=ALU.add), [sq.res], [ss.res])
                        self.act(ss.t[:, 4:8], ss.t[:, 0:4], AF.Ln, [ss.res, self.epsc.res], [ss.res], scale=1.0 / 64, bias=self.epsc.t[:, :])
                        self.act(ss.t[:, 8:12], ss.t[:, 4:8], AF.Exp, [ss.res], [ss.res], scale=-0.5)
                        for gg in range(4):
                            self.stt(self.vg[g].t[:, t, 64 * gg:64 * gg + 64], gv.t[:, 64 * gg:64 * gg + 64], ss.t[:, 8 + gg:9 + gg],
                                     self.sgug.t[:, l, 64 * gg:64 * gg + 64], ALU.mult, ALU.mult,
                                     [gv.res, ss.res, self.sgug.res], [self.vg[g].res])
            self.ring_release()
        for g in range(2 if dbg_b >= 7 else 0):
            c0, n, v = self.GR[g]
            for h in range(4):
                rb = 64 * (h % 2)
                ps = self.bank()
                self.pe(lambda e, ps=ps, h=h, rb=rb, g=g, n=n: e.matmul(ps.t[:, :n], lhsT=self.wukT.t[rb:rb + 64, l, h // 2, :],
                                                                      rhs=self.qnT[g].t[rb:rb + 64, h // 2, :], start=True, stop=True,
                                                                      tile_position=(rb, 0)),
                        [self.wukT.res, self.qnT_res[g][h // 2]], [ps.res])
                self.cp(self.qabsT[g].t[:, h, :], ps.t[:, :n], [ps.res], [self.qabsT[g].res])

    def _attn(self, nq, nchunks, score_fn, score_reads, v_fn, v_reads, scale, depth=2):
        ob = self.abank()
        Es = {}

        def pv(i):
            E = Es.pop(i)
            self.pe(lambda e: e.matmul(ob.t[:, :nq], lhsT=v_fn(i), rhs=E.t[:, :nq], start=(i == 0), stop=(i == nchunks - 1)),
                    [E.res] + v_reads(i), [ob.res])
        for i in range(nchunks):
            sbk = self.bank()
            self.pe(lambda e, i=i, sbk=sbk: score_fn(e, i, sbk.t[:, :nq]), score_reads(i), [sbk.res])
            E = self.et()
            self.act(E.t[:, :nq], sbk.t[:, :nq], AF.Exp, [sbk.res], [E.res], scale=scale)
            Es[i] = E
            if i >= depth:
                pv(i - depth)
        for i in range(max(0, nchunks - depth), nchunks):
            pv(i)
        return ob

    def _attn_stream(self, units, depth=2):
        pend = []

        def pv(u, i, E):
            ob, nq, n = u["ob"], u["nq"], u["n"]
            self.pe(lambda e: e.matmul(ob.t[:, :nq], lhsT=u["v_fn"](i), rhs=E.t[:, :nq], start=(i == 0), stop=(i == n - 1)),
                    [E.res] + u["v_reads"](i), [ob.res])
            if i == n - 1 and u.get("done") is not None:
                u["done"](ob)
        flat = [(u, i) for u in units for i in range(u["n"])]
        for p0 in range(0, len(flat), 2):
            grp = flat[p0:p0 + 2]
            sb_list = []
            for (u, i) in grp:
                if i == 0:
                    u["ob"] = self.abank()
                    if u.get("pre") is not None:
                        u["pre"]()
                nq = u["nq"]
                sbk = self.bank()
                self.pe(lambda e, u=u, i=i, sbk=sbk, nq=nq: u["score_fn"](e, i, sbk.t[:, :nq]), u["score_reads"](i), [sbk.res])
                sb_list.append(sbk)
            for (u, i), sbk in zip(grp, sb_list):
                nq = u["nq"]
                E = self.et()
                self.act(E.t[:, :nq], sbk.t[:, :nq], AF.Exp, [sbk.res], [E.res], scale=u["scale"])
                pend.append((u, i, E))
            while len(pend) > depth:
                pv(*pend.pop(0))
        while pend:
            pv(*pend.pop(0))

    def _recip(self, rd, dr, ob, nq):
        self.act(rd.t[dr, :nq], ob.t[64:128, :nq], AF.Ln, [ob.res], [rd.res])
        self.act(rd.t[dr, :nq], rd.t[dr, :nq], AF.Exp, [rd.res], [rd.res], scale=-1.0)

    def _epi_dense(self, ob, nq, h, g, mchunk, c0):
        dr = slice(64 * (h % 2), 64 * (h % 2) + 64)
        rd = self.ft()
        self._recip(rd, dr, ob, nq)
        self.tt(self.hT.t[dr, mchunk, c0:c0 + nq], ob.t[0:64, :nq], rd.t[dr, :nq], ALU.mult, [ob.res, rd.res], [self.hres[g][mchunk]])

    def _epi_diff_head(self, ob1, ob2, nq, h, l, opair):
        dr = slice(64 * (h % 2), 64 * (h % 2) + 64)
        r1 = self.ft()
        t1 = self.ft()
        self._recip(r1, dr, ob1, nq)
        self.tt(t1.t[dr, :nq], ob1.t[0:64, :nq], r1.t[dr, :nq], ALU.mult, [ob1.res, r1.res], [t1.res])
        r2 = self.ft()
        t2 = self.ft()
        self._recip(r2, dr, ob2, nq)
        self.tt(t2.t[dr, :nq], ob2.t[0:64, :nq], r2.t[dr, :nq], ALU.mult, [ob2.res, r2.res], [t2.res])
        self.stt(opair.t[dr, :nq], t2.t[dr, :nq], self.nlam.t[dr, l:l + 1], t1.t[dr, :nq], ALU.mult, ALU.add,
                 [t1.res, t2.res, self.nlam.res], [opair.res])

    def _epi_diff_pair(self, opair, nq, l, g, mchunk, c0):
        osq = self.bt()
        self.act(osq.t[:, :nq], opair.t[:, :nq], AF.Square, [opair.res], [osq.res])
        ssb = self.bank()
        self.pe(lambda e: e.matmul(ssb.t[:, :nq], lhsT=self.bdones.t[:, :], rhs=osq.t[:, :nq], start=True, stop=True),
                [osq.res, self.bdones.res], [ssb.res])
        rstd = self._rstd_from_ps(ssb, nq, 1.0 / 64)
        self.stt(self.hT.t[:, mchunk, c0:c0 + nq], opair.t[:, :nq], self.gsub.t[:, l:l + 1], rstd.t[:, :nq], ALU.mult, ALU.mult,
                 [opair.res, self.gsub.res, rstd.res], [self.hres[g][mchunk]])

    def _mix_prompt(self, l, after_unit):
        g = 0
        for t in range(4):
            ps = self.bank()
            self.pe(lambda e, ps=ps, t=t: e.matmul(ps.t[:, 0:256], lhsT=self.ckvT_p.t[:, 128 * t:128 * t + 128], rhs=self.wuv.t[:, l, :],
                                                  start=True, stop=True), [self.ckvT_p.res, self.wuv.res], [ps.res])
            self.cp(self.Vp.t[:, t, 2, :, 0:64], ps.t[:, 0:256].rearrange("p (h d) -> p h d", h=4), [ps.res], [self.Vp_res[t][2]], eng="dve")
        units = []
        for s in range(2):
            q0 = 256 * s
            tiles = (2 * s, 2 * s + 1)
            for h in range(4):
                rb = 64 * (h % 2)

                def sc(e, i, out, h=h, rb=rb, tiles=tiles, q0=q0):
                    t = tiles[i]
                    return e.matmul(out, lhsT=self.kaT_p.t[rb:rb + 64, h // 2, 128 * t:128 * t + 128],
                                    rhs=self.qaT[0].t[rb:rb + 64, h // 2, q0:q0 + 256], start=True, stop=True, tile_position=(rb, 0))

                def done(ob, h=h, q0=q0):
                    self._epi_dense(ob, 256, h, g, 0 + h // 2, q0)
                    after_unit()
                units.append(dict(nq=256, n=2, score_fn=sc, score_reads=lambda i, h=h: [self.kaT_res[h // 2], self.qaT_res[0][h // 2]],
                                  v_fn=lambda i, h=h, tiles=tiles: self.Vp.t[:, tiles[i], 0, h, :],
                                  v_reads=lambda i, tiles=tiles: [self.Vp_res[tiles[i]][0]], scale=1.0, done=done))
            for hp in range(2):
                opair = self.opair_t[self.o_rr % 2]
                self.o_rr += 1
                for h in (2 * hp, 2 * hp + 1):
                    pair_units = []
                    for c in range(2):
                        rb = 64 * (h % 2) + 32 * c

                        def sc(e, i, out, h=h, rb=rb, tiles=tiles, q0=q0):
                            t = tiles[i]
                            return e.matmul(out, lhsT=self.kbT_p.t[rb:rb + 32, h // 2, 128 * t:128 * t + 128],
                                            rhs=self.qbT[0].t[rb:rb + 32, h // 2, q0:q0 + 256], start=True, stop=True, tile_position=(rb, 0))
                        u = dict(nq=256, n=2, score_fn=sc, score_reads=lambda i, h=h: [self.kbT_res[h // 2], self.qbT_res[0][h // 2]],
                                 v_fn=lambda i, h=h, tiles=tiles: self.Vp.t[:, tiles[i], 1, h, :],
                                 v_reads=lambda i, tiles=tiles: [self.Vp_res[tiles[i]][1]], scale=32 ** -0.5, done=None)
                        pair_units.append(u)

                    def done(ob, h=h, hp=hp, pu=pair_units, opair=opair, q0=q0):
                        self._epi_diff_head(pu[0]["ob"], pu[1]["ob"], 256, h, l, opair)
                        after_unit()
                        if h == 2 * hp + 1:
                            self._epi_diff_pair(opair, 256, l, g, 2 + hp, q0)
                    pair_units[1]["done"] = done
                    units += pair_units
            for h in range(4):
                def sc(e, i, out, h=h, tiles=tiles, q0=q0):
                    t = tiles[i]
                    e.matmul(out, lhsT=self.ckvT_p.t[:, 128 * t:128 * t + 128], rhs=self.qabsT[0].t[:, h, q0:q0 + 256], start=True, stop=False)
                    return e.matmul(out, lhsT=self.kpe4_p.t[32 * h:32 * h + 32, 128 * t:128 * t + 128],
                                    rhs=self.qpeT[0].t[32 * h:32 * h + 32, q0:q0 + 256], start=False, stop=True, tile_position=(32 * h, 0))

                def done(ob, h=h, q0=q0):
                    self._epi_dense(ob, 256, h, g, 4 + h // 2, q0)
                    after_unit()
                units.append(dict(nq=256, n=2, score_fn=sc,
                                  score_reads=lambda i: [self.ckvT_p.res, self.qabsT[0].res, self.kpe4_p.res, self.qpeT[0].res],
                                  v_fn=lambda i, h=h, tiles=tiles: self.Vp.t[:, tiles[i], 2, h, :],
                                  v_reads=lambda i, tiles=tiles: [self.Vp_res[tiles[i]][2]], scale=96 ** -0.5, done=done))
        self._attn_stream(units)
        self._sgu(l, 0)

    def _sgu(self, l, g):
        c0, n, v = self.GR[g]
        nt = n // 128
        for j in range(2):
            pb = self.bank()

            def fn(e, j=j, pb=pb):
                ins = None
                for t in range(nt):
                    for gg in (2 * j, 2 * j + 1):
                        cb = 64 * (gg % 2)
                        ins = e.matmul(pb.t[cb:cb + 64, 128 * t:128 * t + 128], lhsT=self.vg[g].t[:, t, 64 * gg:64 * gg + 64],
                                       rhs=self.sguw.t[:, l, gg, :], start=True, stop=True, tile_position=(0, cb))
                return ins
            self.pe(fn, [self.vg[g].res, self.sguw.res], [pb.res])
            tmp = self.ft()
            for t in range(nt):
                self.tt(tmp.t[:, 128 * t:128 * t + 128], pb.t[:, 128 * t:128 * t + 128], self.sgub.t[:, l, j, :], ALU.add,
                        [pb.res, self.sgub.res], [tmp.res])
            self.tt(self.hT.t[:, 6 + j, c0:c0 + n], tmp.t[:, :n], self.uT[g].t[:, j, :], ALU.mult, [tmp.res, self.uT[g].res], [self.hres[g][6 + j]])

    def _kres(self, region, i):
        return self.kvk_res[region][0 if i < 4 else 1 + (i - 4) // 2]

    def _load_kv(self, l, region, kcols, vcols, ck_src, cv_src, ck2_src=None, kcols2=None):
        S = self.S
        kvk = self.kvk[region]
        kr = self.kvk_res[region]
        gath = self.gath[l] if ck2_src is None else self.gath2[l]
        gres = self.gath_res[l] if ck2_src is None else self.gath2_res[l]
        if ck_src is None:
            pass
        elif ck2_src is None:
            S.dma("pool", kvk.t[:, :, 0:512], ck_src.rearrange("p (j c) -> p j c", j=2), writes=[kr[0]])
            for r in range(4):
                S.dma("sp", kvk.t[:, :, 512 + 256 * r:512 + 256 * r + 256],
                      gath[128 * r:128 * r + 128, kcols:kcols + 512].rearrange("p (j c) -> p j c", j=2), reads=[gres], writes=[kr[1 + r]])
        else:
            S.dma("pool", kvk.t[:, 0, 0:512], ck_src, writes=[kr[0]])
            S.dma("pool", kvk.t[:, 1, 0:512], ck2_src, writes=[kr[0]])
            for r in range(4):
                S.dma("sp", kvk.t[:, 0, 512 + 256 * r:512 + 256 * r + 256], gath[128 * r:128 * r + 128, kcols:kcols + 256],
                      reads=[gres], writes=[kr[1 + r]])
                S.dma("sp", kvk.t[:, 1, 512 + 256 * r:512 + 256 * r + 256], gath[128 * r:128 * r + 128, kcols2:kcols2 + 256],
                      reads=[gres], writes=[kr[1 + r]])
        if cv_src is not None:
            cv4 = cv_src.rearrange("p (c h d) -> p c h d", c=4, h=4)
            for c in range(4):
                S.dma("pool", self.kvv_h(c)[:, :, 0:64], cv4[:, c], writes=[self.kvv_res[c]])
            for r in range(4):
                gv = gath[128 * r:128 * r + 128, vcols:vcols + 512].rearrange("p (t h d) -> p t h d", t=2, h=4)
                for t in range(2):
                    S.dma("sp", self.kvv_h(4 + 2 * r + t)[:, :, 0:64], gv[:, t], reads=[gres], writes=[self.kvv_res[4 + 2 * r + t]])

    def _mix_sample(self, l, after_unit):
        S = self.S
        g = 1
        nq = NS
        kvk, kvv = self.kvk[0], self.kvvf
        units = []
        for h in range(4):
            rb = 64 * (h % 2)
            G = self.G[self.g_rr % 2]
            self.g_rr += 1

            def pre(h=h, G=G):
                S.dma("pool", G.t[:, :, :], self.i_gbias[l * 4 + h].rearrange("p (c q) -> p c q", c=8), writes=[G.res])

            def sc(e, i, out, h=h, rb=rb, G=G, kvk=kvk):
                if i < 4:
                    return e.matmul(out, lhsT=kvk.t[rb:rb + 64, h // 2, 128 * i:128 * i + 128], rhs=self.qaT[1].t[rb:rb + 64, h // 2, :],
                                    start=True, stop=True, tile_position=(rb, 0))
                e.matmul(out, lhsT=self.identb.t[:, :], rhs=G.t[:, i - 4, :], start=True, stop=False)
                return e.matmul(out, lhsT=kvk.t[rb:rb + 64, h // 2, 128 * i:128 * i + 128], rhs=self.qaT[1].t[rb:rb + 64, h // 2, :],
                                start=False, stop=True, tile_position=(rb, 0))

            def done(ob, h=h):
                self._epi_dense(ob, nq, h, g, h // 2, NP)
                after_unit(6.0)
            units.append(dict(nq=nq, n=12, pre=pre, score_fn=sc,
                              score_reads=lambda i, G=G, h=h: [self._kres(0, i), self.qaT_res[1][h // 2], G.res, self.identb.res],
                              v_fn=lambda i, h=h: self.kvv_lhsT(i, h), v_reads=lambda i: [self.kvv_res[i]], scale=1.0, done=done))
        self._attn_stream(units)
        self._load_kv(l, 1, 512, 1536, None, self.i_cvb[l])
        kvk, kvv = self.kvk[1], self.kvvf
        units = []
        for hp in range(2):
            opair = self.opair_t[self.o_rr % 2]
            self.o_rr += 1
            for h in (2 * hp, 2 * hp + 1):
                pu = []
                for c in range(2):
                    rb = 64 * (h % 2) + 32 * c

                    def sc(e, i, out, h=h, rb=rb, kvk=kvk):
                        return e.matmul(out, lhsT=kvk.t[rb:rb + 32, h // 2, 128 * i:128 * i + 128], rhs=self.qbT[1].t[rb:rb + 32, h // 2, :],
                                        start=True, stop=True, tile_position=(rb, 0))
                    pu.append(dict(nq=nq, n=12, score_fn=sc, score_reads=lambda i, h=h: [self._kres(1, i), self.qbT_res[1][h // 2]],
                                   v_fn=lambda i, h=h: self.kvv_lhsT(i, h), v_reads=lambda i: [self.kvv_res[i]], scale=32 ** -0.5,
                                   done=(lambda ob: after_unit(6.0))))

                def done(ob, h=h, hp=hp, pu=pu, opair=opair):
                    after_unit(6.0)
                    self._epi_diff_head(pu[0]["ob"], pu[1]["ob"], nq, h, l, opair)
                    if h == 2 * hp + 1:
                        self._epi_diff_pair(opair, nq, l, g, 2 + hp, NP)
                pu[1]["done"] = done
                units += pu
        self._attn_stream(units)
        self._load_kv(l, 0, 0, None, self.i_cckvT[l], None, ck2_src=self.i_ckpe4[l], kcols2=256)
        kvk, kvv = self.kvk[0], self.kvvf
        for i in range(12):
            ps = self.bank()
            self.pe(lambda e, ps=ps, i=i: e.matmul(ps.t[:, 0:256], lhsT=kvk.t[:, 0, 128 * i:128 * i + 128], rhs=self.wuv.t[:, l, :],
                                                  start=True, stop=True), [self._kres(0, i), self.wuv.res], [ps.res])
            self.cp(self.kvv_h(i)[:, :, 0:64], ps.t[:, 0:256].rearrange("p (h d) -> p h d", h=4), [ps.res], [self.kvv_res[i]], eng="dve")
        units = []
        for h in range(4):
            def sc(e, i, out, h=h, kvk=kvk):
                e.matmul(out, lhsT=kvk.t[:, 0, 128 * i:128 * i + 128], rhs=self.qabsT[1].t[:, h, :], start=True, stop=False)
                return e.matmul(out, lhsT=kvk.t[32 * h:32 * h + 32, 1, 128 * i:128 * i + 128], rhs=self.qpeT[1].t[32 * h:32 * h + 32, :],
                                start=False, stop=True, tile_position=(32 * h, 0))

            def done(ob, h=h):
                self._epi_dense(ob, nq, h, g, 4 + h // 2, NP)
                after_unit(6.0)
            units.append(dict(nq=nq, n=12, score_fn=sc, score_reads=lambda i: [self._kres(0, i), self.qabsT[1].res, self.qpeT[1].res],
                              v_fn=lambda i, h=h: self.kvv_lhsT(i, h), v_reads=lambda i: [self.kvv_res[i]], scale=96 ** -0.5, done=done))
        self._attn_stream(units)
        self._sgu(l, 1)

    def _out_proj(self, l):
        mod = self.mod[l]
        ws = [self.ring[(self.ring_used + jb) % RING] for jb in range(2)]

        def chunk(g, oc):
            c0, n, v = self.GR[g]
            w = ws[oc // 4]
            o4 = oc % 4
            ps = self.bank()

            def fn(e):
                ins = None
                for k in range(8):
                    ins = e.matmul(ps.t[:, :n], lhsT=w.t[:, 512 * k + 128 * o4:512 * k + 128 * o4 + 128], rhs=self.hT.t[:, k, c0:c0 + n],
                                   start=(k == 0), stop=(k == 7))
                return ins
            self.pe(fn, [w.res] + self.hres[g], [ps.res])
            xs = self.xT.t[:, oc, c0:c0 + n]
            self.stt(xs, ps.t[:, :n], mod.t[:, v, 2, oc:oc + 1], xs, ALU.mult, ALU.add,
                     [ps.res, self.mod_res[l][0], self.xres[g][oc]], [self.xres[g][oc]])
        for oc in range(8):
            chunk(1, oc)
        for oc in range(4):
            chunk(0, oc)
        self._norm(l, 1, 2)
        for oc in range(4, 8):
            chunk(0, oc)
        self._norm(l, 0, 2)
        self.ring_release()
        self.ring_release()

    def _ffn(self, l):
        mod = self.mod[l]
        TG = ((384, 768), (0, 384))

        def hreads(a, b):
            r = []
            if a < NP:
                r += self.hres[0]
            if b > NP:
                r += self.hres[1]
            return r
        for fb in range(8):
            w1 = self.ring_acquire()
            for (a, b) in TG:
                n = b - a
                for fc in range(4):
                    ps = self.bank()

                    def fn(e, ps=ps, fc=fc, a=a, b=b, n=n):
                        ins = None
                        for k in range(8):
                            ins = e.matmul(ps.t[:, :n], lhsT=w1.t[:, 512 * k + 128 * fc:512 * k + 128 * fc + 128], rhs=self.hT.t[:, k, a:b],
                                           start=(k == 0), stop=(k == 7))
                        return ins
                    self.pe(fn, [w1.res] + hreads(a, b), [ps.res])
                    r = self.ft()
                    self.act(r.t[:, :n], ps.t[:, :n], AF.Relu, [ps.res], [r.res])
                    tgi = 0 if a == 0 else 1
                    self.tt(self.uF(fc, a, b), r.t[:, :n], r.t[:, :n], ALU.mult, [r.res],
                            [self.uF_res[tgi][fc]] + (self.kvk_res[0] if fb == 0 else []))
            self.ring_release()
            w2 = self.ring_acquire()
            last = (fb == 7 and l + 1 < L and self.stage >= 99)

            def chunk(a, b, oc):
                n = b - a
                ps = self.bank()

                def fn(e):
                    ins = None
                    for fc in range(4):
                        ins = e.matmul(ps.t[:, :n], lhsT=w2.t[:, 1024 * fc + 128 * oc:1024 * fc + 128 * oc + 128], rhs=self.uF(fc, a, b),
                                       start=(fc == 0), stop=(fc == 3))
                    return ins
                tgi = 0 if a == 0 else 1
                self.pe(fn, [w2.res] + self.uF_res[tgi] + (self.kvk_res[0] if fb == 7 else []), [ps.res])
                for (g, lo, hi) in ((0, a, min(b, NP)), (1, max(a, NP), b)):
                    if hi <= lo:
                        continue
                    v = self.GR[g][2]
                    xs = self.xT.t[:, oc, lo:hi]
                    self.stt(xs, ps.t[:, lo - a:hi - a], mod.t[:, v, 5, oc:oc + 1], xs, ALU.mult, ALU.add,
                             [ps.res, self.mod_res[l][1], self.xres[g][oc]], [self.xres[g][oc]])
            for oc in range(8):
                chunk(384, 768, oc)
            for oc in range(4):
                chunk(0, 384, oc)
            if last:
                self._norm(l + 1, 1, 1)
            for oc in range(4, 8):
                chunk(0, 384, oc)
            if last:
                self._norm(l + 1, 0, 1)
                self.norm1_done = l + 1
            self.ring_release()

    def _layer(self, l):
        st = self.stage
        if st < 2:
            return
        if getattr(self, "norm1_done", -1) != l:
            for g in (1, 0):
                self._norm(l, g, 1)
        if st < 3:
            return
        self._in_proj(l)
        if st < 4:
            return
        todo = [(l + 1, j) for j in range(3)] if l + 1 < L else []
        state = {"u": 0.0}

        def do_one():
            ll, j = todo.pop(0)
            self._ada_block(ll, j)
            if j == 2:
                self._ada_exchange(ll)

        def after_unit(wt=1.0):
            state["u"] += wt
            while todo and state["u"] >= 8.0:
                state["u"] -= 8.0
                do_one()
        if st >= 5:
            self._load_kv(l, 0, 0, 1024, self.i_ckaT[l], self.i_cva[l])
            self._load_kv(l, 1, 512, None, self.i_ckbT[l], None)
        self._mix_prompt(l, after_unit)
        if st >= 5:
            self._mix_sample(l, after_unit)
        while todo:
            do_one()
        if l + 1 < L:
            self._ada_finish(l + 1, 0)
            self._ada_finish(l + 1, 1)
        if st < 6:
            return
        self._out_proj(l)
        if st < 7:
            return
        self._ffn(l)

    def _final(self):
        gf = [self.stg[0], self.stg[1]]
        for half in range(2):
            self.S.dma("sp", gf[half].t[:, :], self.i_gfin[:, 512 * half:512 * half + 512], writes=[gf[half].res])
        for g in range(2):
            c0, n, v = self.GR[g]
            dst = self.o_yp if g == 0 else self.o_ys
            for t in range(n // 128):
                a = c0 + 128 * t
                halves = []
                ss = self.smt()
                for half in range(2):
                    ps = self.bank()

                    def fn(e, ps=ps, half=half, a=a):
                        ins = None
                        for kk in range(4):
                            k = 4 * half + kk
                            ins = e.transpose(ps.t[:, 128 * kk:128 * kk + 128], self.xT.t[:, k, a:a + 128], self.ident.t[:, :])
                        return ins
                    self.pe(fn, [self.ident.res] + self.xres[g][4 * half:4 * half + 4], [ps.res])
                    xh = self.ft()
                    self.cp(xh.t[:, :], ps.t[:, :], [ps.res], [xh.res], eng="dve")
                    junk = self.bt()
                    self.act(junk.t[:, :], xh.t[:, :], AF.Square, [xh.res], [junk.res, ss.res], accum=ss.t[:, half:half + 1])
                    halves.append(xh)
                self.tt(ss.t[:, 2:3], ss.t[:, 0:1], ss.t[:, 1:2], ALU.add, [ss.res], [ss.res])
                self.act(ss.t[:, 3:4], ss.t[:, 2:3], AF.Ln, [ss.res, self.epsc.res], [ss.res], scale=1.0 / D, bias=self.epsc.t[:, :])
                self.act(ss.t[:, 4:5], ss.t[:, 3:4], AF.Exp, [ss.res], [ss.res], scale=-0.5)
                for half in range(2):
                    xh = halves[half]
                    self.stt(xh.t[:, :], xh.t[:, :], ss.t[:, 4:5], gf[half].t[:, :], ALU.mult, ALU.mult,
                             [xh.res, ss.res, gf[half].res], [xh.res])
                    self._out_dma(dst[128 * t:128 * t + 128, 512 * half:512 * half + 512], xh.t[:, :], xh.res)


OFF = dict(qa=0, ka=256, va=512, qb=768, kb=1024, vb=1280, qc=1536, ckv=1920, kpe=2048, u=2080, vs=2336)


def _swap_idx(n):
    i = np.arange(n)
    return np.where((i % 16) < 8, i + 8, i - 8)


def _blk8(wcols):
    n = wcols.shape[1]
    out = np.zeros((128, 8, 512), np.float32)
    out[:, :, :n] = wcols.reshape(8, 128, n).transpose(1, 0, 2)
    return out.reshape(128, 4096)


def _weight_blocks(inp):
    blocks = []
    w_ada, w_in, w_out, w_ff1, w_ff2 = (np.asarray(inp[k], np.float32) for k in ("w_ada", "w_in", "w_out", "w_ff1", "w_ff2"))

    def ada(l):
        return [_blk8(w_ada[l][:, 512 * j:512 * j + 512]) for j in range(12)]

    def inb(l):
        W = w_in[l]
        c = lambda name, n: W[:, OFF[name]:OFF[name] + n]
        qb, kb = c("qb", 256), c("kb", 256)
        qc = c("qc", 384).reshape(1024, 4, 96)
        qn = qc[:, :, :64].reshape(1024, 256)
        qpe = qc[:, :, 64:].reshape(1024, 128)
        kpe = c("kpe", 32)
        kpe4 = np.tile(kpe, (1, 4))
        sw256 = _swap_idx(256)
        sw128 = _swap_idx(128)
        return [
            _blk8(np.concatenate([c("u", 256), c("vs", 256)], 1)),
            _blk8(np.concatenate([c("qa", 256), c("ka", 256)], 1)),
            _blk8(np.concatenate([qb, qb[:, sw256]], 1)),
            _blk8(np.concatenate([kb, kb[:, sw256]], 1)),
            _blk8(np.concatenate([c("va", 256), c("vb", 256)], 1)),
            _blk8(np.concatenate([qn, c("ckv", 128)], 1)),
            _blk8(np.concatenate([qpe, qpe[:, sw128], kpe4, kpe4[:, sw128]], 1)),
        ]

    def outb(l):
        return [_blk8(w_out[l][:, 512 * j:512 * j + 512]) for j in range(2)]

    def ffnb(l):
        r = []
        for fb in range(8):
            r.append(_blk8(w_ff1[l][:, 512 * fb:512 * fb + 512]))
            w2 = w_ff2[l][512 * fb:512 * fb + 512, :].reshape(4, 128, 1024).transpose(1, 0, 2)
            r.append(np.ascontiguousarray(w2).reshape(128, 4096))
        return r

    for l in range(L):
        blocks += inb(l)
        blocks += outb(l)
        blocks += ffnb(l)
    assert len(blocks) == L * NBLK_L
    wada = [np.stack([ada(l)[3 * r + j] for l in range(L) for j in range(3)], 0) for r in range(4)]
    return np.stack(blocks, 0), wada


def _rope_tables(rank):
    f = np.arange(128) % 32
    is_col = f >= 16
    j = (f % 16) % 8
    first = (f % 16) < 8
    freqs = (np.float32(10000.0) ** (-np.arange(8, dtype=np.float32) / np.float32(8))).astype(np.float32)
    t = 256 * rank + np.arange(256)
    rows = (t // 64).astype(np.float32)
    cols = (t % 64).astype(np.float32)
    pos = np.where(is_col[:, None], cols[None, :], rows[None, :]).astype(np.float32)
    ang = (pos * freqs[j][:, None]).astype(np.float32)
    C = np.cos(ang).astype(np.float32)
    Sn = np.sin(ang).astype(np.float32)
    return C, np.where(first[:, None], -Sn, Sn).astype(np.float32)


def _gbias(rpb, rank):
    kr = np.arange(2)[:, None, None, None, None]
    kcol = np.arange(64)[None, :, None, None, None]
    c = np.arange(8)[None, None, :, None, None]
    qr = np.arange(4)[None, None, None, :, None]
    qcol = np.arange(64)[None, None, None, None, :]
    krow = 2 * c + kr
    qrow = 4 * rank + qr
    r0 = np.clip(qrow - 4, 0, 8)
    vrow = (krow >= r0) & (krow < r0 + 8)
    c0 = np.clip(qcol - 8, 0, 48)
    vcol = (kcol >= c0) & (kcol < c0 + 16)
    valid = np.broadcast_to(vrow & vcol, (2, 64, 8, 4, 64))
    di = np.broadcast_to(np.clip(krow - qrow + 7, 0, 14), (2, 64, 8, 4, 64))
    dj = np.broadcast_to(np.clip(kcol - qcol, -15, 15) + 15, (2, 64, 8, 4, 64))
    g = rpb[:, :, di, dj]
    g = np.where(valid[None, None], g, np.float32(NEG)).astype(np.float32)
    return g.reshape(L * 4, 128, 8 * 256)


def _prep_shared(inp):
    f = lambda k: np.asarray(inp[k], np.float32)
    sh = {}
    sh["wblk"], sh["_wada"] = _weight_blocks(inp)
    bada = f("b_ada").reshape(L, 48, 128).transpose(2, 0, 1)
    sh["bada"] = np.ascontiguousarray(bada).reshape(128, L * 48)
    sh["_badap"] = [np.ascontiguousarray(bada[:, :, 12 * r:12 * r + 12]).reshape(128, L * 12) for r in range(4)]
    sh["gmix"] = np.ascontiguousarray(f("g_mix").reshape(L, 8, 128).transpose(2, 0, 1)).reshape(128, L * 8)
    sh["gffn"] = np.ascontiguousarray(f("g_ffn").reshape(L, 8, 128).transpose(2, 0, 1)).reshape(128, L * 8)
    lqk = np.stack([f("diff_lq1"), f("diff_lq2"), f("diff_lk1"), f("diff_lk2")], 1)
    sh["lqk"] = np.ascontiguousarray(np.broadcast_to(lqk.reshape(1, -1), (128, L * 128)))
    gs = f("diff_g_subln")
    sh["gsub"] = np.ascontiguousarray(np.concatenate([gs, gs], 1).T)
    gk = f("mla_g_ckv")
    sh["gckvf"] = np.ascontiguousarray(gk.T)
    sh["gckvb"] = np.ascontiguousarray(np.broadcast_to(gk.reshape(1, -1), (128, L * 128)))
    wuk = f("mla_w_uk")
    wukT = wuk.reshape(L, 128, 2, 2, 64).transpose(3, 4, 0, 2, 1)
    sh["wukT"] = np.ascontiguousarray(wukT).reshape(128, L * 2 * 128)
    sh["wuv"] = np.ascontiguousarray(f("mla_w_uv").transpose(1, 0, 2)).reshape(128, L * 256)
    sh["sgug"] = np.ascontiguousarray(np.broadcast_to(f("sgu_g").reshape(1, -1), (128, L * 256)))
    sh["sguw"] = np.ascontiguousarray(f("sgu_w").transpose(3, 0, 1, 2)).reshape(128, L * 4 * 128)
    sb = f("sgu_b").reshape(L, 2, 2, 128)
    sb = np.broadcast_to(sb[:, :, :, None, :], (L, 2, 2, 64, 128)).transpose(2, 3, 0, 1, 4)
    sh["sgub"] = np.ascontiguousarray(sb).reshape(128, L * 2 * 128)
    sh["gfin"] = np.ascontiguousarray(np.broadcast_to(f("g_final").reshape(1, -1), (128, 1024)))
    sh["ident"] = np.eye(128, dtype=np.float32)
    return sh


def _prep_core(inp, c, sh, rank_cache):
    f = lambda k: np.asarray(inp[k], np.float32)
    sbatch, rank = c // 4, c % 4
    m = {k: v for k, v in sh.items() if not k.startswith("_")}
    m["wada"] = sh["_wada"][rank]
    m["badap"] = sh["_badap"][rank]
    xp = f("x_prompt")[2 * c:2 * c + 2].reshape(NP, D)
    xs = f("x_sample")[sbatch, 256 * rank:256 * rank + 256]
    x = np.concatenate([xp, xs], 0)
    m["xT"] = np.ascontiguousarray(x.T.reshape(8, 128, NT).transpose(1, 0, 2)).reshape(128, 8 * NT)
    mv = np.stack([f("c_ctx"), f("c")[sbatch]], 1)
    m["mT"] = np.ascontiguousarray(mv.reshape(8, 128, 2).transpose(1, 0, 2)).reshape(128, 16)
    if rank not in rank_cache:
        C, Sg = _rope_tables(rank)
        rank_cache[rank] = (C, Sg, _gbias(f("na_rpb"), rank))
    m["ropeC"], m["ropeS"], m["gbias"] = rank_cache[rank]
    key = ("b", sbatch)
    if key not in rank_cache:
        def kT(a):
            return np.ascontiguousarray(a.reshape(L, 2, 2, 512, 64).transpose(0, 2, 4, 1, 3)).reshape(L, 128, 1024)

        def vT(a):
            return np.ascontiguousarray(a.reshape(L, 4, 4, 128, 64).transpose(0, 3, 2, 1, 4)).reshape(L, 128, 1024)
        ckv = f("cache_mla_ckv")[sbatch]
        kpe = f("cache_mla_kpe")[sbatch]
        rank_cache[key] = dict(
            ckaT=kT(f("cache_na_k")[sbatch]), ckbT=kT(f("cache_diff_k")[sbatch]),
            cva=vT(f("cache_na_v")[sbatch]), cvb=vT(f("cache_diff_v")[sbatch]),
            cckvT=np.ascontiguousarray(ckv.transpose(0, 2, 1)),
            ckpe4=np.ascontiguousarray(np.tile(kpe.transpose(0, 2, 1), (1, 4, 1))))
    m.update(rank_cache[key])
    return m


_NC_CACHE = {}


def kernel(**inputs):
    if "nc" not in _NC_CACHE:
        _NC_CACHE["nc"] = Prog().build()
    nc = _NC_CACHE["nc"]
    sh = _prep_shared(inputs)
    cache = {}
    in_maps = [_prep_core(inputs, c, sh, cache) for c in range(8)]
    res = run_bass_kernel_spmd(nc, in_maps, core_ids=list(range(8)))
    R = res.results
    y_prompt = np.stack([R[c]["yp"].reshape(2, 256, D) for c in range(8)], 0).reshape(16, 256, D)
    y_sample = np.stack([np.concatenate([R[4 * b + r]["ys"] for r in range(4)], 0) for b in range(2)], 0)
    cat = lambda k: np.concatenate([R[c][k] for c in range(8)], 0)
    outs = (y_prompt, y_sample, cat("onak"), cat("onav"), cat("odk"), cat("odv"), cat("ockv"), cat("okpe"))
    return tuple(np.ascontiguousarray(o, dtype=np.float32) for o in outs)
```

```python
import math
from contextlib import ExitStack
import numpy as np
import concourse.bass as bass
import concourse.mybir as mybir
from concourse.bass_utils import run_bass_kernel_spmd

F32 = mybir.dt.float32
BF16 = mybir.dt.bfloat16
AF = mybir.ActivationFunctionType
ALU = mybir.AluOpType
AX = mybir.AxisListType

L = 4
D = 1024
NP = 512
NS = 256
NT = NP + NS
EPS = 1e-6
NEG = -30000.0
NBLK_L = 25
RING = 4
SAME_SYNC = {"pe": False, "act": True, "dve": True, "pool": False, "sp": False}


def lam_init_of(l):
    return 0.8 - 0.6 * math.exp(-0.3 * l)


class Res:
    __slots__ = ("name", "w", "r", "x")

    def __init__(self, name, x=False):
        self.name = name
        self.w = None
        self.r = {}
        self.x = x


class Buf:
    __slots__ = ("t", "res")

    def __init__(self, t, res):
        self.t = t
        self.res = res


class Sched:
    def __init__(self, nc, es):
        self.nc = nc
        self.es = es
        self.E = {"pe": nc.tensor, "act": nc.scalar, "dve": nc.vector, "pool": nc.gpsimd, "sp": nc.sync}
        self.sem = {k: es.enter_context(nc.semaphore("prog_" + k)) for k in ("pe", "act", "dve")}
        self.cnt = {k: 0 for k in self.E}
        self.known = {k: {} for k in self.E}
        self.dsem = {q: [es.enter_context(nc.semaphore("d_%s%d" % (q, i))) for i in range(n)]
                     for q, n in (("pool", 10), ("sp", 8))}
        self.dval = {}
        self.drr = {"pool": 0, "sp": 0}
        self.ccsem = es.enter_context(nc.semaphore("ccsem"))
        self.ccn = 0
        self.out_toks = []

    def _wait(self, e, toks):
        need = {}
        for (sem, val, eng) in toks:
            if eng == e and not SAME_SYNC[e]:
                continue
            k = id(sem)
            if self.known[e].get(k, 0) >= val:
                continue
            if k not in need or need[k][1] < val:
                need[k] = (sem, val)
        for sem, val in need.values():
            self.E[e].wait_ge(sem, val)
            self.known[e][id(sem)] = val

    @staticmethod
    def _deps(reads, writes):
        toks = []
        for r in reads:
            if r.w is not None:
                toks.append(r.w)
            if r.x:
                toks.extend(r.r.values())
        for w in writes:
            if w.w is not None:
                toks.append(w.w)
            toks.extend(w.r.values())
        return toks

    def op(self, e, fn, reads=(), writes=()):
        self._wait(e, self._deps(reads, writes))
        ins = fn(self.E[e])
        self.cnt[e] += 1
        ins.then_inc(self.sem[e], 1)
        tok = (self.sem[e], self.cnt[e], e)
        for r in reads:
            r.r[e] = tok
        for w in writes:
            w.w = tok
            w.r = {}
        return tok

    def dma(self, q, out, in_, reads=(), writes=(), is_output=False):
        sems = self.dsem[q]
        sem = sems[self.drr[q] % len(sems)]
        self.drr[q] += 1
        v = self.dval.get(id(sem), 0)
        toks = self._deps(reads, writes)
        if v > 0:
            toks.append((sem, v, None))
        self._wait(q, toks)
        self.E[q].dma_start(out=out, in_=in_).then_inc(sem, 16)
        self.dval[id(sem)] = v + 16
        tok = (sem, v + 16, None)
        key = ("d", id(sem))
        for r in reads:
            r.r[key] = tok
        for w in writes:
            w.w = tok
            w.r = {}
        if is_output:
            self.out_toks.append(tok)
        return tok

    def allgather(self, in_ap, out_ap, reads, writes):
        q = "pool"
        self._wait(q, self._deps(reads, writes))
        ins = self.nc.gpsimd.collective_compute("AllGather", ALU.bypass,
                                                replica_groups=[[0, 1, 2, 3], [4, 5, 6, 7]],
                                                ins=[in_ap.opt()], outs=[out_ap.opt()])
        ins.then_inc(self.ccsem)
        self.ccn += 1
        tok = (self.ccsem, self.ccn, None)
        key = ("cc",)
        for r in reads:
            r.r[key] = tok
        for w in writes:
            w.w = tok
            w.r = {}
        return tok

    def finish(self):
        toks = list(self.out_toks)
        for q in self.dsem:
            for sem in self.dsem[q]:
                v = self.dval.get(id(sem), 0)
                if v > 0:
                    toks.append((sem, v, None))
        if self.ccn > 0:
            toks.append((self.ccsem, self.ccn, None))
        for e in ("pe", "act", "dve"):
            if self.cnt[e] > 0:
                toks.append((self.sem[e], self.cnt[e], e))
        self._wait("sp", toks)


class Prog:
    def __init__(self):
        self.nc = bass.Bass("TRN2", target_bir_lowering=False)
        self.es = ExitStack()

    def din(self, name, shape, dt=F32):
        return self.nc.dram_tensor(name, list(shape), dt, kind="ExternalInput").ap()

    def dout(self, name, shape, dt=F32):
        return self.nc.dram_tensor(name, list(shape), dt, kind="ExternalOutput").ap()

    def sb(self, name, shape, dt):
        t = self.es.enter_context(self.nc.sbuf_tensor("sb_" + name, list(shape), dt))
        return Buf(t, Res(name))

    def build(self, stage=99, nlayers=L):
        nc = self.nc
        es = self.es
        self.stage = stage
        self.nblk = L * NBLK_L
        with es:
            self.S = Sched(nc, es)
            self._declare_io()
            self._alloc()
            self._setup()
            if stage >= 1:
                for j in range(3):
                    self._ada_block(0, j)
                self._ada_exchange(0)
                self._ada_finish(0, 0)
                self._ada_finish(0, 1)
            for l in range(nlayers):
                self._layer(l)
            if stage >= 9:
                self._final()
            self.S.finish()
        return nc

    def _declare_io(self):
        d = self.din
        self.i_xT = d("xT", [128, 8 * NT])
        self.i_mT = d("mT", [128, 16])
        self.i_w = d("wblk", [self.nblk, 128, 4096])
        self.i_wa = d("wada", [L * 3, 128, 4096])
        self.i_badap = d("badap", [128, L * 12])
        self.bounce_e = [self.nc.dram_tensor("bouncee%d" % l, [128, 24], F32).ap() for l in range(L)]
        self.gath_e = [self.nc.dram_tensor("gathe%d" % l, [512, 24], F32).ap() for l in range(L)]
        self.bounce_e_res = [Res("bouncee%d" % l) for l in range(L)]
        self.gath_e_res = [Res("gathe%d" % l) for l in range(L)]
        self.blocks = [self.i_wa[j] for j in range(3)]
        for l in range(L):
            self.blocks += [self.i_w[NBLK_L * l + j] for j in range(7)]
            if l + 1 < L:
                self.blocks += [self.i_wa[3 * (l + 1) + j] for j in range(3)]
            self.blocks += [self.i_w[NBLK_L * l + 7 + j] for j in range(18)]
        self.i_bada = d("bada", [128, L * 48])
        self.i_gmix = d("gmix", [128, L * 8])
        self.i_gffn = d("gffn", [128, L * 8])
        self.i_lqk = d("lqk", [128, L * 4 * 32])
        self.i_gsub = d("gsub", [128, L])
        self.i_gckvf = d("gckvf", [128, L])
        self.i_gckvb = d("gckvb", [128, L * 128])
        self.i_wukT = d("wukT", [128, L * 2 * 128])
        self.i_wuv = d("wuv", [128, L * 256])
        self.i_sgug = d("sgug", [128, L * 256])
        self.i_sguw = d("sguw", [128, L * 4 * 128])
        self.i_sgub = d("sgub", [128, L * 2 * 128])
        self.i_gfin = d("gfin", [128, 1024])
        self.i_ident = d("ident", [128, 128])
        self.i_ropeC = d("ropeC", [128, 256])
        self.i_ropeS = d("ropeS", [128, 256])
        self.i_gbias = d("gbias", [L * 4, 128, 8 * 256])
        self.i_ckaT = d("ckaT", [L, 128, 2 * 512])
        self.i_ckbT = d("ckbT", [L, 128, 2 * 512])
        self.i_cva = d("cva", [L, 128, 4 * 256])
        self.i_cvb = d("cvb", [L, 128, 4 * 256])
        self.i_cckvT = d("cckvT", [L, 128, 512])
        self.i_ckpe4 = d("ckpe4", [L, 128, 512])
        o = self.dout
        self.o_yp = o("yp", [NP, D])
        self.o_ys = o("ys", [NS, D])
        self.o_nak = o("onak", [2, L, 4, 256, 64])
        self.o_nav = o("onav", [2, L, 4, 256, 64])
        self.o_dk = o("odk", [2, L, 4, 256, 64])
        self.o_dv = o("odv", [2, L, 4, 256, 64])
        self.o_ckv = o("ockv", [2, L, 256, 128])
        self.o_kpe = o("okpe", [2, L, 256, 32])
        self.bounce = [self.nc.dram_tensor("bounce%d" % l, [128, 2048], BF16).ap() for l in range(L)]
        self.gath = [self.nc.dram_tensor("gath%d" % l, [512, 2048], BF16).ap() for l in range(L)]
        self.bounce2 = [self.nc.dram_tensor("bounceb%d" % l, [128, 512], BF16).ap() for l in range(L)]
        self.gath2 = [self.nc.dram_tensor("gathb%d" % l, [512, 512], BF16).ap() for l in range(L)]
        self.bounce2_res = [Res("bounceb%d" % l) for l in range(L)]
        self.gath2_res = [Res("gathb%d" % l) for l in range(L)]
        self.bounce_res = [Res("bounce%d" % l) for l in range(L)]
        self.gath_res = [Res("gath%d" % l) for l in range(L)]

    def _alloc(self):
        sb = self.sb
        nc = self.nc
        self.xT = sb("xT_sb", [128, 8, NT], F32)
        self.hT = sb("hT_sb", [128, 8, NT], BF16)
        self.xres = [[Res("x%d_%d" % (g, k)) for k in range(8)] for g in range(2)]
        self.hres = [[Res("h%d_%d" % (g, k)) for k in range(8)] for g in range(2)]
        self.ring = [sb("ring%d" % i, [128, 4096], BF16) for i in range(RING)]
        self.ring_issued = 0
        self.ring_used = 0
        self.banks = []
        for i in range(8):
            t = self.es.enter_context(nc.psum_tensor("ps%d" % i, [128, 512], F32))
            self.banks.append(Buf(t, Res("ps%d" % i, x=True)))
        self.gen_rr = 0
        self.acc_rr = 0
        self.ident = sb("ident", [128, 128], F32)
        self.identb = sb("identb", [128, 128], BF16)
        self.ones = sb("ones", [128, 128], BF16)
        self.bdones = sb("bdones", [128, 128], BF16)
        self.epsc = sb("epsc", [128, 1], F32)
        self.mT = sb("mT", [128, 16], F32)
        self.sT = sb("sT", [128, 16], BF16)
        self.bada = sb("bada", [128, L, 48], F32)
        self.badap = sb("badap", [128, L, 12], F32)
        self.epart = sb("epart", [128, L, 24], F32)
        self.gmix = sb("gmix", [128, L, 8], F32)
        self.gffn = sb("gffn", [128, L, 8], F32)
        self.lqk = sb("lqk", [128, L, 4, 32], F32)
        self.lqp = sb("lqp", [128, L, 2, 32], F32)
        self.lqe = sb("lqe", [128, L, 2], F32)
        self.nlam = sb("nlam", [128, L], F32)
        self.gsub = sb("gsub", [128, L], F32)
        self.gckvf = sb("gckvf", [128, L], F32)
        self.gckvb = sb("gckvb", [128, L, 128], F32)
        self.wukT = sb("wukT", [128, L, 2, 128], BF16)
        self.wuv = sb("wuv", [128, L, 256], BF16)
        self.sgug = sb("sgug", [128, L, 256], F32)
        self.sguw = sb("sguw", [128, L, 4, 128], BF16)
        self.sgub = sb("sgub", [128, L, 2, 128], F32)
        self.ropeC = sb("ropeC", [128, 256], F32)
        self.ropeS = sb("ropeS", [128, 256], F32)
        self.esb = [sb("esb%d" % l, [128, 48, 2], F32) for l in range(L)]
        self.mod = [sb("mod%d" % l, [128, 2, 6, 8], F32) for l in range(L)]
        self.mod_res = [[Res("mod%d_%d" % (l, p)) for p in range(2)] for l in range(L)]
        self.qaT = [sb("qaT_p", [128, 2, NP], BF16), sb("qaT_s", [128, 2, NS], BF16)]
        self.qbT = [sb("qbT_p", [128, 2, NP], BF16), sb("qbT_s", [128, 2, NS], BF16)]
        self.qnT = [sb("qnT_p", [128, 2, NP], BF16), sb("qnT_s", [128, 2, NS], BF16)]
        self.qaT_res = [[Res("qaT%d_%d" % (g, j)) for j in range(2)] for g in range(2)]
        self.qbT_res = [[Res("qbT%d_%d" % (g, j)) for j in range(2)] for g in range(2)]
        self.qnT_res = [[Res("qnT%d_%d" % (g, j)) for j in range(2)] for g in range(2)]
        self.kaT_res = [Res("kaT_p%d" % j) for j in range(2)]
        self.kbT_res = [Res("kbT_p%d" % j) for j in range(2)]
        self.qpeT = [sb("qpeT_p", [128, NP], BF16), sb("qpeT_s", [128, NS], BF16)]
        self.qabsT = [sb("qabsT_p", [128, 4, NP], BF16), sb("qabsT_s", [128, 4, NS], BF16)]
        self.uT = [sb("uT_p", [128, 2, NP], BF16), sb("uT_s", [128, 2, NS], BF16)]
        self.vg = [sb("vg_p", [128, 4, 256], BF16), sb("vg_s", [128, 2, 256], BF16)]
        self.kaT_p = sb("kaT_p", [128, 2, NP], BF16)
        self.kbT_p = sb("kbT_p", [128, 2, NP], BF16)
        self.ckvT_p = sb("ckvT_p", [128, NP], BF16)
        self.kpe4_p = sb("kpe4_p", [128, NP], BF16)
        self.Vp = sb("Vp", [128, 4, 3, 4, 128], BF16)
        self.Vp_res = [[Res("Vp%d_%d" % (t, m)) for m in range(3)] for t in range(4)]
        self.xb = sb("xb", [128, 2560], BF16)
        self.xb_res = {k: Res("xb_" + k) for k in ("ka0", "ka1", "kb0", "kb1", "ckv", "kpe", "va0", "va1", "vb0", "vb1")}
        self.kvk = [sb("kvk%d" % i, [128, 2, 1536], BF16) for i in range(2)]
        self.kvvf = sb("kvvf", [128, 12 * 512], BF16)
        self.kvk_res = [[Res("kvk%d_%d" % (i, k)) for k in range(5)] for i in range(2)]
        self.uF_res = [[Res("uF%d_%d" % (t, fc)) for fc in range(4)] for t in range(2)]
        self.kvv_res = [Res("kvv_%d" % i) for i in range(12)]
        self.G = [sb("G%d" % i, [128, 8, 256], BF16) for i in range(2)]
        self.g_rr = 0
        self.E = [sb("E%d" % i, [128, 512], BF16) for i in range(4)]
        self.e_rr = 0
        self.f32t = [sb("f32t%d" % i, [128, 512], F32) for i in range(6)]
        self.f_rr = 0
        self.rstd_t = [sb("rstdt%d" % i, [128, 512], F32) for i in range(2)]
        self.r_rr = 0
        self.opair_t = [sb("opair%d" % i, [128, 512], F32) for i in range(2)]
        self.o_rr = 0
        self.b16t = [sb("b16t%d" % i, [128, 512], BF16) for i in range(2)]
        self.b_rr = 0
        self.stg = [sb("stg%d" % i, [128, 512], F32) for i in range(3)]
        self.s_rr = 0
        self.small = [sb("small%d" % i, [128, 16], F32) for i in range(6)]
        self.sm_rr = 0
        self.evac_rr = 0

    def kvv_h(self, i):
        return self.kvvf.t[:, 512 * i:512 * i + 512].rearrange("p (h c) -> p h c", h=4)

    def kvv_lhsT(self, i, h):
        return self.kvvf.t[:, 512 * i + 128 * h:512 * i + 128 * h + 128]

    def uF(self, fc, a, b):
        o = NT * (fc % 2)
        return self.kvk[0].t[:, fc // 2, o + a:o + b]

    def bank(self):
        b = self.banks[self.gen_rr % 4]
        self.gen_rr += 1
        return b

    def abank(self):
        b = self.banks[4 + self.acc_rr % 3]
        self.acc_rr += 1
        return b

    def ft(self):
        b = self.f32t[self.f_rr % len(self.f32t)]
        self.f_rr += 1
        return b

    def bt(self):
        b = self.b16t[self.b_rr % len(self.b16t)]
        self.b_rr += 1
        return b

    def et(self):
        b = self.E[self.e_rr % len(self.E)]
        self.e_rr += 1
        return b

    def st(self):
        b = self.stg[self.s_rr % len(self.stg)]
        self.s_rr += 1
        return b

    def smt(self):
        b = self.small[self.sm_rr % len(self.small)]
        self.sm_rr += 1
        return b

    def act(self, out, in_, func, reads, writes, scale=1.0, bias=None, accum=None):
        kw = {}
        if bias is not None:
            kw["bias"] = bias
        if accum is not None:
            kw["accum_out"] = accum
        return self.S.op("act", lambda e: e.activation(out=out, in_=in_, func=func, scale=scale, **kw), reads, writes)

    def tt(self, out, in0, in1, op, reads, writes):
        return self.S.op("dve", lambda e: e.tensor_tensor(out=out, in0=in0, in1=in1, op=op), reads, writes)

    def ts(self, out, in0, s1, op0, reads, writes, s2=None, op1=None):
        if op1 is None:
            return self.S.op("dve", lambda e: e.tensor_scalar(out=out, in0=in0, scalar1=s1, scalar2=None, op0=op0), reads, writes)
        return self.S.op("dve", lambda e: e.tensor_scalar(out=out, in0=in0, scalar1=s1, scalar2=s2, op0=op0, op1=op1), reads, writes)

    def stt(self, out, in0, scalar, in1, op0, op1, reads, writes):
        return self.S.op("dve", lambda e: e.scalar_tensor_tensor(out=out, in0=in0, scalar=scalar, in1=in1, op0=op0, op1=op1), reads, writes)

    def cp(self, out, in_, reads, writes, eng=None, scale=None):
        if eng is None:
            eng = "act" if (self.evac_rr % 2 == 0) else "dve"
            self.evac_rr += 1
        if eng == "act":
            return self.act(out, in_, AF.Copy, reads, writes, scale=(1.0 if scale is None else scale))
        if scale is None:
            return self.S.op("dve", lambda e: e.tensor_copy(out=out, in_=in_), reads, writes)
        return self.ts(out, in_, scale, ALU.mult, reads, writes)

    def pe(self, fn, reads, writes):
        return self.S.op("pe", fn, reads, writes)

    def ring_issue(self):
        i = self.ring_issued
        if i >= len(self.blocks):
            return
        slot = self.ring[i % RING]
        self.S.dma("pool", slot.t[:, :], self.blocks[i], reads=(), writes=[slot.res])
        self.ring_issued += 1

    def ring_acquire(self):
        i = self.ring_used
        assert i < self.ring_issued
        return self.ring[i % RING]

    def ring_release(self):
        self.ring_used += 1
        self.ring_issue()

    def _setup(self):
        S = self.S
        for _ in range(RING):
            self.ring_issue()

        def ld(q, buf, src):
            S.dma(q, buf.t[:], src, reads=(), writes=[buf.res])

        def v2(ap, pat, **kw):
            return ap.rearrange(pat, **kw)

        ld("sp", self.mT, self.i_mT)
        S.dma("sp", self.bada.t[:], v2(self.i_bada, "p (l c) -> p l c", l=L), writes=[self.bada.res])
        S.dma("sp", self.badap.t[:], v2(self.i_badap, "p (l c) -> p l c", l=L), writes=[self.badap.res])
        S.dma("sp", self.gmix.t[:], v2(self.i_gmix, "p (l c) -> p l c", l=L), writes=[self.gmix.res])
        S.dma("sp", self.gffn.t[:], v2(self.i_gffn, "p (l c) -> p l c", l=L), writes=[self.gffn.res])
        for g, (c0, n) in enumerate(((0, NP), (NP, NS))):
            src = self.i_xT.rearrange("p (k t) -> p k t", k=8)
            S.dma("sp", self.xT.t[:, :, c0:c0 + n], src[:, :, c0:c0 + n], writes=self.xres[g])
        ld("sp", self.ident, self.i_ident)
        S.dma("pool", self.identb.t[:], self.i_ident, writes=[self.identb.res])
        S.dma("sp", self.lqk.t[:], v2(self.i_lqk, "p (l a c) -> p l a c", l=L, a=4), writes=[self.lqk.res])
        ld("sp", self.gsub, self.i_gsub)
        ld("sp", self.gckvf, self.i_gckvf)
        S.dma("sp", self.gckvb.t[:], v2(self.i_gckvb, "p (l c) -> p l c", l=L), writes=[self.gckvb.res])
        S.dma("pool", self.wukT.t[:], v2(self.i_wukT, "p (l a c) -> p l a c", l=L, a=2), writes=[self.wukT.res])
        S.dma("pool", self.wuv.t[:], v2(self.i_wuv, "p (l c) -> p l c", l=L), writes=[self.wuv.res])
        S.dma("sp", self.sgug.t[:], v2(self.i_sgug, "p (l c) -> p l c", l=L), writes=[self.sgug.res])
        S.dma("pool", self.sguw.t[:], v2(self.i_sguw, "p (l a c) -> p l a c", l=L, a=4), writes=[self.sguw.res])
        S.dma("sp", self.sgub.t[:], v2(self.i_sgub, "p (l a c) -> p l a c", l=L, a=2), writes=[self.sgub.res])
        ld("sp", self.ropeC, self.i_ropeC)
        ld("sp", self.ropeS, self.i_ropeS)
        S.op("dve", lambda e: e.memset(self.ones.t[:], 1.0), (), [self.ones.res])
        S.op("dve", lambda e: e.memset(self.bdones.t[:], 0.0), (), [self.bdones.res])
        S.op("dve", lambda e: e.memset(self.bdones.t[0:64, 0:64], 1.0), (), [self.bdones.res])
        S.op("dve", lambda e: e.memset(self.bdones.t[64:128, 64:128], 1.0), (), [self.bdones.res])
        S.op("dve", lambda e: e.memset(self.epsc.t[:], EPS), (), [self.epsc.res])
        for t in range(4):
            for m in range(3):
                S.op("dve", lambda e, t=t, m=m: e.memset(self.Vp.t[:, t, m, :, 64:128], 1.0), (), [self.Vp_res[t][m]])
        for i in range(12):
            S.op("dve", lambda e, i=i: e.memset(self.kvv_h(i)[:, :, 64:128], 1.0), (), [self.kvv_res[i]])
        self.act(self.sT.t[:], self.mT.t[:], AF.Silu, [self.mT.res], [self.sT.res])
        self.tt(self.lqp.t[:], self.lqk.t[:, :, 0:2, :], self.lqk.t[:, :, 2:4, :], ALU.mult, [self.lqk.res], [self.lqp.res])
        S.op("dve", lambda e: e.tensor_reduce(out=self.lqe.t[:], in_=self.lqp.t[:], axis=AX.X, op=ALU.add),
             [self.lqp.res], [self.lqe.res])
        self.act(self.lqe.t[:], self.lqe.t[:], AF.Exp, [self.lqe.res], [self.lqe.res])
        for l in range(L):
            self.ts(self.nlam.t[:, l:l + 1], self.lqe.t[:, l, 1:2], self.lqe.t[:, l, 0:1], ALU.subtract,
                    [self.lqe.res], [self.nlam.res], s2=-lam_init_of(l), op1=ALU.add)
            self.ts(self.gsub.t[:, l:l + 1], self.gsub.t[:, l:l + 1], 1.0 - lam_init_of(l), ALU.mult,
                    [self.gsub.res], [self.gsub.res])

    def _ada_block(self, l, j):
        w = self.ring_acquire()
        ps = self.banks[7]
        cb = 24 * (l % 2)

        def fn(e):
            ins = None
            for cc in range(4):
                ch = 4 * j + cc
                for k in range(8):
                    ins = e.matmul(ps.t[:, cb + 2 * ch:cb + 2 * ch + 2], lhsT=w.t[:, 512 * k + 128 * cc:512 * k + 128 * cc + 128],
                                   rhs=self.sT.t[:, 2 * k:2 * k + 2], start=(k == 0), stop=(k == 7))
            return ins
        self.pe(fn, [w.res, self.sT.res], [ps.res])
        self.ring_release()

    def _ada_exchange(self, l):
        S = self.S
        ps = self.banks[7]
        cb = 24 * (l % 2)
        ep = self.epart
        for v in range(2):
            self.tt(ep.t[:, l, :].rearrange("p (c v) -> p c v", v=2)[:, :, v],
                    ps.t[:, cb:cb + 24].rearrange("p (c v) -> p c v", v=2)[:, :, v], self.badap.t[:, l, :], ALU.add,
                    [ps.res, self.badap.res], [ep.res])
        S.dma("pool", self.bounce_e[l], ep.t[:, l, :], reads=[ep.res], writes=[self.bounce_e_res[l]])
        S.allgather(self.bounce_e[l], self.gath_e[l], reads=[self.bounce_e_res[l]], writes=[self.gath_e_res[l]])
        esb = self.esb[l]
        for r in range(4):
            S.dma("sp", esb.t[:, 12 * r:12 * r + 12, :], self.gath_e[l][128 * r:128 * r + 128, :].rearrange("p (c v) -> p c v", v=2),
                  reads=[self.gath_e_res[l]], writes=[esb.res])

    def _ada_finish(self, l, part):
        esb = self.esb[l]
        mod = self.mod[l]
        mres = self.mod_res[l][part]
        c0 = 24 * part
        gvec = self.gmix if part == 0 else self.gffn
        for v in range(2):
            self.cp(mod.t[:, v, 3 * part + 0, :], esb.t[:, c0:c0 + 8, v], [esb.res], [mres], eng="dve")
            self.stt(mod.t[:, v, 3 * part + 1, :], esb.t[:, c0 + 8:c0 + 16, v], 1.0, gvec.t[:, l, :], ALU.add, ALU.mult,
                     [esb.res, gvec.res], [mres])
            self.cp(mod.t[:, v, 3 * part + 2, :], esb.t[:, c0 + 16:c0 + 24, v], [esb.res], [mres], eng="dve")

    GR = ((0, NP, 0), (NP, NS, 1))

    def _rstd_from_ps(self, ps, n, inv_count, rows=slice(0, 128)):
        r = self.rstd_t[self.r_rr % 2]
        self.r_rr += 1
        self.act(r.t[rows, :n], ps.t[rows, :n], AF.Ln, [ps.res, self.epsc.res], [r.res], scale=inv_count, bias=self.epsc.t[rows, :])
        self.act(r.t[rows, :n], r.t[rows, :n], AF.Exp, [r.res], [r.res], scale=-0.5)
        return r

    def _norm(self, l, g, which):
        c0, n, v = self.GR[g]
        sh_i, gsc_i = (0, 1) if which == 1 else (3, 4)
        mod = self.mod[l]
        mres = self.mod_res[l][0 if which == 1 else 1]
        ssb = self.bank()
        for k in range(8):
            sq = self.bt()
            self.act(sq.t[:, :n], self.xT.t[:, k, c0:c0 + n], AF.Square, [self.xres[g][k]], [sq.res])
            self.pe(lambda e, k=k, sq=sq: e.matmul(ssb.t[:, :n], lhsT=self.ones.t[:, :], rhs=sq.t[:, :n], start=(k == 0), stop=(k == 7)),
                    [sq.res, self.ones.res], [ssb.res])
        import os
        dn = int(os.environ.get("DBG_NORM", "9"))
        if dn < 2:
            return
        rstd = self._rstd_from_ps(ssb, n, 1.0 / D)
        if dn < 3:
            return
        for k in range(8):
            t = self.ft()
            self.stt(t.t[:, :n], self.xT.t[:, k, c0:c0 + n], mod.t[:, v, gsc_i, k:k + 1], rstd.t[:, :n], ALU.mult, ALU.mult,
                     [self.xres[g][k], mres, rstd.res], [t.res])
            if dn < 4:
                continue
            self.act(self.hT.t[:, k, c0:c0 + n], t.t[:, :n], AF.Identity, [t.res, mres], [self.hres[g][k]],
                     bias=mod.t[:, v, sh_i, k:k + 1])

    def _fm(self, g, w, col0, M):
        c0, n, _ = self.GR[g]
        ps = self.bank()

        def fn(e):
            ins = None
            for k in range(8):
                ins = e.matmul(ps.t[0:M, :n], lhsT=w.t[:, 512 * k + col0:512 * k + col0 + M], rhs=self.hT.t[:, k, c0:c0 + n],
                               start=(k == 0), stop=(k == 7))
            return ins
        self.pe(fn, [w.res] + self.hres[g], [ps.res])
        return ps

    def _tm(self, g, t, w, col0, N):
        c0, n, _ = self.GR[g]
        ps = self.bank()
        a = c0 + 128 * t

        def fn(e):
            ins = None
            for k in range(8):
                ins = e.matmul(ps.t[:, :N], lhsT=self.hT.t[:, k, a:a + 128], rhs=w.t[:, 512 * k + col0:512 * k + col0 + N],
                               start=(k == 0), stop=(k == 7))
            return ins
        self.pe(fn, [w.res] + self.hres[g], [ps.res])
        return ps

    def _rope(self, out, out_res, psA, psB, n):
        t1 = self.ft()
        t2 = self.ft()
        self.tt(t1.t[:, :n], psA.t[:, :n], self.ropeC.t[:, :n], ALU.mult, [psA.res, self.ropeC.res], [t1.res])
        self.tt(t2.t[:, :n], psB.t[:, :n], self.ropeS.t[:, :n], ALU.mult, [psB.res, self.ropeS.res], [t2.res])
        self.tt(out, t1.t[:, :n], t2.t[:, :n], ALU.add, [t1.res, t2.res], [out_res])

    def _out_dma(self, dst, src_ap, src_res):
        self.S.dma("sp", dst, src_ap, reads=[src_res], writes=(), is_output=True)

    def _hd(self, o, s, l, r0):
        return o[s, l].rearrange("h s d -> s h d")[r0:r0 + 128]

    def _in_proj(self, l):
        S = self.S
        xb = self.xb
        import os
        dbg_b = int(os.environ.get("DBG_B", "9"))
        dbg_g = int(os.environ.get("DBG_G", "2"))
        for b in (6, 0, 1, 2, 3, 4, 5):
            w = self.ring_acquire()
            for g in (1, 0):
                if b > dbg_b or g >= dbg_g:
                    continue
                c0, n, v = self.GR[g]
                isP = (g == 0)
                if b == 0:
                    for j in range(2):
                        ps = self._fm(g, w, 128 * j, 128)
                        self.cp(self.qaT[g].t[:, j, :], ps.t[:, :n], [ps.res], [self.qaT_res[g][j]], scale=0.125)
                    for j in range(2):
                        ps = self._fm(g, w, 256 + 128 * j, 128)
                        if isP:
                            self.cp(self.kaT_p.t[:, j, :], ps.t[:, :n], [ps.res], [self.kaT_res[j]])
                        else:
                            self.cp(xb.t[:, 256 * j:256 * j + 256], ps.t[:, :n], [ps.res], [self.xb_res["ka%d" % j]])
                    if isP:
                        for t in range(4):
                            ps = self._tm(g, t, w, 256, 256)
                            st = self.st()
                            self.cp(st.t[:, 0:256], ps.t[:, 0:256], [ps.res], [st.res])
                            self._out_dma(self._hd(self.o_nak, t // 2, l, 128 * (t % 2)),
                                          st.t[:, 0:256].rearrange("p (h d) -> p h d", h=4), st.res)
                elif b in (1, 2):
                    for j in range(2):
                        psA = self._fm(g, w, 128 * j, 128)
                        if isP:
                            dst = self.qbT[0] if b == 1 else self.kbT_p
                            dres = self.qbT_res[0][j] if b == 1 else self.kbT_res[j]
                            self.cp(dst.t[:, j, :], psA.t[:, :n], [psA.res], [dres])
                        else:
                            psB = self._fm(g, w, 256 + 128 * j, 128)
                            if b == 1:
                                self._rope(self.qbT[1].t[:, j, :], self.qbT_res[1][j], psA, psB, n)
                            else:
                                self._rope(xb.t[:, 512 + 256 * j:512 + 256 * j + 256], self.xb_res["kb%d" % j], psA, psB, n)
                    if isP and b == 2:
                        for t in range(4):
                            ps = self._tm(g, t, w, 0, 256)
                            st = self.st()
                            self.cp(st.t[:, 0:256], ps.t[:, 0:256], [ps.res], [st.res])
                            self._out_dma(self._hd(self.o_dk, t // 2, l, 128 * (t % 2)),
                                          st.t[:, 0:256].rearrange("p (h d) -> p h d", h=4), st.res)
                elif b == 3:
                    dx = int(os.environ.get("DBG_X", "9"))
                    for t in range(n // 128):
                        ps = self._tm(g, t, w, 0, 512)
                        if dx < 2:
                            continue
                        if isP:
                            st = self.st()
                            self.cp(st.t[:, :], ps.t[:, :], [ps.res], [st.res])
                            if dx >= 3:
                                self._out_dma(self._hd(self.o_nav, t // 2, l, 128 * (t % 2)),
                                              st.t[:, 0:256].rearrange("p (h d) -> p h d", h=4), st.res)
                                self._out_dma(self._hd(self.o_dv, t // 2, l, 128 * (t % 2)),
                                              st.t[:, 256:512].rearrange("p (h d) -> p h d", h=4), st.res)
                            if dx >= 4:
                                for m in range(2):
                                    self.cp(self.Vp.t[:, t, m, :, 0:64], ps.t[:, 256 * m:256 * m + 256].rearrange("p (h d) -> p h d", h=4),
                                            [ps.res], [self.Vp_res[t][m]])
                        else:
                            self.cp(xb.t[:, 1024 + 256 * t:1024 + 256 * t + 256], ps.t[:, 0:256], [ps.res], [self.xb_res["va%d" % t]])
                            self.cp(xb.t[:, 1536 + 256 * t:1536 + 256 * t + 256], ps.t[:, 256:512], [ps.res], [self.xb_res["vb%d" % t]])
                    if not isP:
                        r1 = [self.xb_res[k] for k in ("ka0", "ka1", "kb0", "kb1", "va0", "va1", "vb0", "vb1")]
                        S.dma("pool", self.bounce[l], xb.t[:, 0:2048], reads=r1, writes=[self.bounce_res[l]])
                        S.allgather(self.bounce[l], self.gath[l], reads=[self.bounce_res[l]], writes=[self.gath_res[l]])
                elif b == 4:
                    for j in range(2):
                        ps = self._fm(g, w, 128 * j, 128)
                        self.cp(self.qnT[g].t[:, j, :], ps.t[:, :n], [ps.res], [self.qnT_res[g][j]])
                    ps = self._fm(g, w, 256, 128)
                    craw = self.ft()
                    sq = self.bt()
                    self.act(craw.t[:, :n], ps.t[:, :n], AF.Copy, [ps.res], [craw.res])
                    self.act(sq.t[:, :n], ps.t[:, :n], AF.Square, [ps.res], [sq.res])
                    ssb = self.bank()
                    self.pe(lambda e, sq=sq, ssb=ssb, n=n: e.matmul(ssb.t[:, :n], lhsT=self.ones.t[:, :], rhs=sq.t[:, :n], start=True, stop=True),
                            [sq.res, self.ones.res], [ssb.res])
                    rstd = self._rstd_from_ps(ssb, n, 1.0 / 128)
                    if isP:
                        dst_ap, dst_res = self.ckvT_p.t[:, :], self.ckvT_p.res
                    else:
                        dst_ap, dst_res = xb.t[:, 2048:2304], self.xb_res["ckv"]
                    self.stt(dst_ap, craw.t[:, :n], self.gckvf.t[:, l:l + 1], rstd.t[:, :n], ALU.mult, ALU.mult,
                             [craw.res, self.gckvf.res, rstd.res], [dst_res])
                    if isP:
                        for t in range(4):
                            ps = self._tm(g, t, w, 256, 128)
                            junk = self.ft()
                            ss = self.smt()
                            self.act(junk.t[:, 0:128], ps.t[:, 0:128], AF.Square, [ps.res], [junk.res, ss.res], accum=ss.t[:, 0:1])
                            self.act(ss.t[:, 1:2], ss.t[:, 0:1], AF.Ln, [ss.res, self.epsc.res], [ss.res], scale=1.0 / 128, bias=self.epsc.t[:, :])
                            self.act(ss.t[:, 2:3], ss.t[:, 1:2], AF.Exp, [ss.res], [ss.res], scale=-0.5)
                            st = self.st()
                            self.stt(st.t[:, 0:128], ps.t[:, 0:128], ss.t[:, 2:3], self.gckvb.t[:, l, :], ALU.mult, ALU.mult,
                                     [ps.res, ss.res, self.gckvb.res], [st.res])
                            self._out_dma(self.o_ckv[t // 2, l, 128 * (t % 2):128 * (t % 2) + 128, :], st.t[:, 0:128], st.res)
                elif b == 5:
                    if isP:
                        ps = self._fm(g, w, 0, 128)
                        self.cp(self.qpeT[0].t[:, :], ps.t[:, :n], [ps.res], [self.qpeT[0].res])
                        ps = self._fm(g, w, 256, 128)
                        self.cp(self.kpe4_p.t[:, :], ps.t[:, :n], [ps.res], [self.kpe4_p.res])
                        for t in range(4):
                            ps = self._tm(g, t, w, 256, 32)
                            st = self.st()
                            self.cp(st.t[:, 0:32], ps.t[:, 0:32], [ps.res], [st.res])
                            self._out_dma(self.o_kpe[t // 2, l, 128 * (t % 2):128 * (t % 2) + 128, :], st.t[:, 0:32], st.res)
                    else:
                        psA = self._fm(g, w, 0, 128)
                        psB = self._fm(g, w, 128, 128)
                        self._rope(self.qpeT[1].t[:, :], self.qpeT[1].res, psA, psB, n)
                        psA = self._fm(g, w, 256, 128)
                        psB = self._fm(g, w, 384, 128)
                        self._rope(xb.t[:, 2304:2560], self.xb_res["kpe"], psA, psB, n)
                        S.dma("pool", self.bounce2[l], xb.t[:, 2048:2560], reads=[self.xb_res["ckv"], self.xb_res["kpe"]], writes=[self.bounce2_res[l]])
                        S.allgather(self.bounce2[l], self.gath2[l], reads=[self.bounce2_res[l]], writes=[self.gath2_res[l]])
                else:
                    for j in range(2):
                        ps = self._fm(g, w, 128 * j, 128)
                        self.act(self.uT[g].t[:, j, :], ps.t[:, :n], AF.Gelu_apprx_tanh, [ps.res], [self.uT[g].res])
                    for t in range(n // 128):
                        ps = self._tm(g, t, w, 256, 256)
                        gv = self.ft()
                        sq = self.ft()
                        ss = self.smt()
                        self.act(gv.t[:, 0:256], ps.t[:, 0:256], AF.Gelu_apprx_tanh, [ps.res], [gv.res])
                        self.tt(sq.t[:, 0:256], gv.t[:, 0:256], gv.t[:, 0:256], ALU.mult, [gv.res], [sq.res])
                        S.op("dve", lambda e, ss=ss, sq=sq: e.tensor_reduce(out=ss.t[:, 0:4], in_=sq.t[:, 0:256].rearrange("p (g c) -> p g c", g=4),
                                                                         axis=AX.X, op=ALU.add), [sq.res], [ss.res])
                        self.act(ss.t[:, 4:8], ss.t[:, 0:4], AF.Ln, [ss.res, self.epsc.res], [ss.res], scale=1.0 / 64, bias=self.epsc.t[:, :])
                        self.act(ss.t[:, 8:12], ss.t[:, 4:8], AF.Exp, [ss.res], [ss.res], scale=-0.5)
                        for gg in range(4):
                            self.stt(self.vg[g].t[:, t, 64 * gg:64 * gg + 64], gv.t[:, 64 * gg:64 * gg + 64], ss.t[:, 8 + gg:9 + gg],
                                     self.sgug.t[:, l, 64 * gg:64 * gg + 64], ALU.mult, ALU.mult,
                                     [gv.res, ss.res, self.sgug.res], [self.vg[g].res])
            self.ring_release()
        for g in range(2 if dbg_b >= 7 else 0):
            c0, n, v = self.GR[g]
            for h in range(4):
                rb = 64 * (h % 2)
                ps = self.bank()
                self.pe(lambda e, ps=ps, h=h, rb=rb, g=g, n=n: e.matmul(ps.t[:, :n], lhsT=self.wukT.t[rb:rb + 64, l, h // 2, :],
                                                                      rhs=self.qnT[g].t[rb:rb + 64, h // 2, :], start=True, stop=True,
                                                                      tile_position=(rb, 0)),
                        [self.wukT.res, self.qnT_res[g][h // 2]], [ps.res])
                self.cp(self.qabsT[g].t[:, h, :], ps.t[:, :n], [ps.res], [self.qabsT[g].res])

    def _attn(self, nq, nchunks, score_fn, score_reads, v_fn, v_reads, scale, depth=2):
        ob = self.abank()
        Es = {}

        def pv(i):
            E = Es.pop(i)
            self.pe(lambda e: e.matmul(ob.t[:, :nq], lhsT=v_fn(i), rhs=E.t[:, :nq], start=(i == 0), stop=(i == nchunks - 1)),
                    [E.res] + v_reads(i), [ob.res])
        for i in range(nchunks):
            sbk = self.bank()
            self.pe(lambda e, i=i, sbk=sbk: score_fn(e, i, sbk.t[:, :nq]), score_reads(i), [sbk.res])
            E = self.et()
            self.act(E.t[:, :nq], sbk.t[:, :nq], AF.Exp, [sbk.res], [E.res], scale=scale)
            Es[i] = E
            if i >= depth:
                pv(i - depth)
        for i in range(max(0, nchunks - depth), nchunks):
            pv(i)
        return ob

    def _attn_stream(self, units, depth=2):
        pend = []

        def pv(u, i, E):
            ob, nq, n = u["ob"], u["nq"], u["n"]
            self.pe(lambda e: e.matmul(ob.t[:, :nq], lhsT=u["v_fn"](i), rhs=E.t[:, :nq], start=(i == 0), stop=(i == n - 1)),
                    [E.res] + u["v_reads"](i), [ob.res])
            if i == n - 1 and u.get("done") is not None:
                u["done"](ob)
        flat = [(u, i) for u in units for i in range(u["n"])]
        for p0 in range(0, len(flat), 2):
            grp = flat[p0:p0 + 2]
            sb_list = []
            for (u, i) in grp:
                if i == 0:
                    u["ob"] = self.abank()
                    if u.get("pre") is not None:
                        u["pre"]()
                nq = u["nq"]
                sbk = self.bank()
                self.pe(lambda e, u=u, i=i, sbk=sbk, nq=nq: u["score_fn"](e, i, sbk.t[:, :nq]), u["score_reads"](i), [sbk.res])
                sb_list.append(sbk)
            for (u, i), sbk in zip(grp, sb_list):
                nq = u["nq"]
                E = self.et()
                self.act(E.t[:, :nq], sbk.t[:, :nq], AF.Exp, [sbk.res], [E.res], scale=u["scale"])
                pend.append((u, i, E))
            while len(pend) > depth:
                pv(*pend.pop(0))
        while pend:
            pv(*pend.pop(0))

    def _recip(self, rd, dr, ob, nq):
        self.act(rd.t[dr, :nq], ob.t[64:128, :nq], AF.Ln, [ob.res], [rd.res])
        self.act(rd.t[dr, :nq], rd.t[dr, :nq], AF.Exp, [rd.res], [rd.res], scale=-1.0)

    def _epi_dense(self, ob, nq, h, g, mchunk, c0):
        dr = slice(64 * (h % 2), 64 * (h % 2) + 64)
        rd = self.ft()
        self._recip(rd, dr, ob, nq)
        self.tt(self.hT.t[dr, mchunk, c0:c0 + nq], ob.t[0:64, :nq], rd.t[dr, :nq], ALU.mult, [ob.res, rd.res], [self.hres[g][mchunk]])

    def _epi_diff_head(self, ob1, ob2, nq, h, l, opair):
        dr = slice(64 * (h % 2), 64 * (h % 2) + 64)
        r1 = self.ft()
        t1 = self.ft()
        self._recip(r1, dr, ob1, nq)
        self.tt(t1.t[dr, :nq], ob1.t[0:64, :nq], r1.t[dr, :nq], ALU.mult, [ob1.res, r1.res], [t1.res])
        r2 = self.ft()
        t2 = self.ft()
        self._recip(r2, dr, ob2, nq)
        self.tt(t2.t[dr, :nq], ob2.t[0:64, :nq], r2.t[dr, :nq], ALU.mult, [ob2.res, r2.res], [t2.res])
        self.stt(opair.t[dr, :nq], t2.t[dr, :nq], self.nlam.t[dr, l:l + 1], t1.t[dr, :nq], ALU.mult, ALU.add,
                 [t1.res, t2.res, self.nlam.res], [opair.res])

    def _epi_diff_pair(self, opair, nq, l, g, mchunk, c0):
        osq = self.bt()
        self.act(osq.t[:, :nq], opair.t[:, :nq], AF.Square, [opair.res], [osq.res])
        ssb = self.bank()
        self.pe(lambda e: e.matmul(ssb.t[:, :nq], lhsT=self.bdones.t[:, :], rhs=osq.t[:, :nq], start=True, stop=True),
                [osq.res, self.bdones.res], [ssb.res])
        rstd = self._rstd_from_ps(ssb, nq, 1.0 / 64)
        self.stt(self.hT.t[:, mchunk, c0:c0 + nq], opair.t[:, :nq], self.gsub.t[:, l:l + 1], rstd.t[:, :nq], ALU.mult, ALU.mult,
                 [opair.res, self.gsub.res, rstd.res], [self.hres[g][mchunk]])

    def _mix_prompt(self, l, after_unit):
        g = 0
        for t in range(4):
            ps = self.bank()
            self.pe(lambda e, ps=ps, t=t: e.matmul(ps.t[:, 0:256], lhsT=self.ckvT_p.t[:, 128 * t:128 * t + 128], rhs=self.wuv.t[:, l, :],
                                                  start=True, stop=True), [self.ckvT_p.res, self.wuv.res], [ps.res])
            self.cp(self.Vp.t[:, t, 2, :, 0:64], ps.t[:, 0:256].rearrange("p (h d) -> p h d", h=4), [ps.res], [self.Vp_res[t][2]], eng="dve")
        units = []
        for s in range(2):
            q0 = 256 * s
            tiles = (2 * s, 2 * s + 1)
            for h in range(4):
                rb = 64 * (h % 2)

                def sc(e, i, out, h=h, rb=rb, tiles=tiles, q0=q0):
                    t = tiles[i]
                    return e.matmul(out, lhsT=self.kaT_p.t[rb:rb + 64, h // 2, 128 * t:128 * t + 128],
                                    rhs=self.qaT[0].t[rb:rb + 64, h // 2, q0:q0 + 256], start=True, stop=True, tile_position=(rb, 0))

                def done(ob, h=h, q0=q0):
                    self._epi_dense(ob, 256, h, g, 0 + h // 2, q0)
                    after_unit()
                units.append(dict(nq=256, n=2, score_fn=sc, score_reads=lambda i, h=h: [self.kaT_res[h // 2], self.qaT_res[0][h // 2]],
                                  v_fn=lambda i, h=h, tiles=tiles: self.Vp.t[:, tiles[i], 0, h, :],
                                  v_reads=lambda i, tiles=tiles: [self.Vp_res[tiles[i]][0]], scale=1.0, done=done))
            for hp in range(2):
                opair = self.opair_t[self.o_rr % 2]
                self.o_rr += 1
                for h in (2 * hp, 2 * hp + 1):
                    pair_units = []
                    for c in range(2):
                        rb = 64 * (h % 2) + 32 * c

                        def sc(e, i, out, h=h, rb=rb, tiles=tiles, q0=q0):
                            t = tiles[i]
                            return e.matmul(out, lhsT=self.kbT_p.t[rb:rb + 32, h // 2, 128 * t:128 * t + 128],
                                            rhs=self.qbT[0].t[rb:rb + 32, h // 2, q0:q0 + 256], start=True, stop=True, tile_position=(rb, 0))
                        u = dict(nq=256, n=2, score_fn=sc, score_reads=lambda i, h=h: [self.kbT_res[h // 2], self.qbT_res[0][h // 2]],
                                 v_fn=lambda i, h=h, tiles=tiles: self.Vp.t[:, tiles[i], 1, h, :],
                                 v_reads=lambda i, tiles=tiles: [self.Vp_res[tiles[i]][1]], scale=32 ** -0.5, done=None)
                        pair_units.append(u)

                    def done(ob, h=h, hp=hp, pu=pair_units, opair=opair, q0=q0):
                        self._epi_diff_head(pu[0]["ob"], pu[1]["ob"], 256, h, l, opair)
                        after_unit()
                        if h == 2 * hp + 1:
                            self._epi_diff_pair(opair, 256, l, g, 2 + hp, q0)
                    pair_units[1]["done"] = done
                    units += pair_units
            for h in range(4):
                def sc(e, i, out, h=h, tiles=tiles, q0=q0):
                    t = tiles[i]
                    e.matmul(out, lhsT=self.ckvT_p.t[:, 128 * t:128 * t + 128], rhs=self.qabsT[0].t[:, h, q0:q0 + 256], start=True, stop=False)
                    return e.matmul(out, lhsT=self.kpe4_p.t[32 * h:32 * h + 32, 128 * t:128 * t + 128],
                                    rhs=self.qpeT[0].t[32 * h:32 * h + 32, q0:q0 + 256], start=False, stop=True, tile_position=(32 * h, 0))

                def done(ob, h=h, q0=q0):
                    self._epi_dense(ob, 256, h, g, 4 + h // 2, q0)
                    after_unit()
                units.append(dict(nq=256, n=2, score_fn=sc,
                                  score_reads=lambda i: [self.ckvT_p.res, self.qabsT[0].res, self.kpe4_p.res, self.qpeT[0].res],
                                  v_fn=lambda i, h=h, tiles=tiles: self.Vp.t[:, tiles[i], 2, h, :],
                                  v_reads=lambda i, tiles=tiles: [self.Vp_res[tiles[i]][2]], scale=96 ** -0.5, done=done))
        self._attn_stream(units)
        self._sgu(l, 0)

    def _sgu(self, l, g):
        c0, n, v = self.GR[g]
        nt = n // 128
        for j in range(2):
            pb = self.bank()

            def fn(e, j=j, pb=pb):
                ins = None
                for t in range(nt):
                    for gg in (2 * j, 2 * j + 1):
                        cb = 64 * (gg % 2)
                        ins = e.matmul(pb.t[cb:cb + 64, 128 * t:128 * t + 128], lhsT=self.vg[g].t[:, t, 64 * gg:64 * gg + 64],
                                       rhs=self.sguw.t[:, l, gg, :], start=True, stop=True, tile_position=(0, cb))
                return ins
            self.pe(fn, [self.vg[g].res, self.sguw.res], [pb.res])
            tmp = self.ft()
            for t in range(nt):
                self.tt(tmp.t[:, 128 * t:128 * t + 128], pb.t[:, 128 * t:128 * t + 128], self.sgub.t[:, l, j, :], ALU.add,
                        [pb.res, self.sgub.res], [tmp.res])
            self.tt(self.hT.t[:, 6 + j, c0:c0 + n], tmp.t[:, :n], self.uT[g].t[:, j, :], ALU.mult, [tmp.res, self.uT[g].res], [self.hres[g][6 + j]])

    def _kres(self, region, i):
        return self.kvk_res[region][0 if i < 4 else 1 + (i - 4) // 2]

    def _load_kv(self, l, region, kcols, vcols, ck_src, cv_src, ck2_src=None, kcols2=None):
        S = self.S
        kvk = self.kvk[region]
        kr = self.kvk_res[region]
        gath = self.gath[l] if ck2_src is None else self.gath2[l]
        gres = self.gath_res[l] if ck2_src is None else self.gath2_res[l]
        if ck_src is None:
            pass
        elif ck2_src is None:
            S.dma("pool", kvk.t[:, :, 0:512], ck_src.rearrange("p (j c) -> p j c", j=2), writes=[kr[0]])
            for r in range(4):
                S.dma("sp", kvk.t[:, :, 512 + 256 * r:512 + 256 * r + 256],
                      gath[128 * r:128 * r + 128, kcols:kcols + 512].rearrange("p (j c) -> p j c", j=2), reads=[gres], writes=[kr[1 + r]])
        else:
            S.dma("pool", kvk.t[:, 0, 0:512], ck_src, writes=[kr[0]])
            S.dma("pool", kvk.t[:, 1, 0:512], ck2_src, writes=[kr[0]])
            for r in range(4):
                S.dma("sp", kvk.t[:, 0, 512 + 256 * r:512 + 256 * r + 256], gath[128 * r:128 * r + 128, kcols:kcols + 256],
                      reads=[gres], writes=[kr[1 + r]])
                S.dma("sp", kvk.t[:, 1, 512 + 256 * r:512 + 256 * r + 256], gath[128 * r:128 * r + 128, kcols2:kcols2 + 256],
                      reads=[gres], writes=[kr[1 + r]])
        if cv_src is not None:
            cv4 = cv_src.rearrange("p (c h d) -> p c h d", c=4, h=4)
            for c in range(4):
                S.dma("pool", self.kvv_h(c)[:, :, 0:64], cv4[:, c], writes=[self.kvv_res[c]])
            for r in range(4):
                gv = gath[128 * r:128 * r + 128, vcols:vcols + 512].rearrange("p (t h d) -> p t h d", t=2, h=4)
                for t in range(2):
                    S.dma("sp", self.kvv_h(4 + 2 * r + t)[:, :, 0:64], gv[:, t], reads=[gres], writes=[self.kvv_res[4 + 2 * r + t]])

    def _mix_sample(self, l, after_unit):
        S = self.S
        g = 1
        nq = NS
        kvk, kvv = self.kvk[0], self.kvvf
        units = []
        for h in range(4):
            rb = 64 * (h % 2)
            G = self.G[self.g_rr % 2]
            self.g_rr += 1

            def pre(h=h, G=G):
                S.dma("pool", G.t[:, :, :], self.i_gbias[l * 4 + h].rearrange("p (c q) -> p c q", c=8), writes=[G.res])

            def sc(e, i, out, h=h, rb=rb, G=G, kvk=kvk):
                if i < 4:
                    return e.matmul(out, lhsT=kvk.t[rb:rb + 64, h // 2, 128 * i:128 * i + 128], rhs=self.qaT[1].t[rb:rb + 64, h // 2, :],
                                    start=True, stop=True, tile_position=(rb, 0))
                e.matmul(out, lhsT=self.identb.t[:, :], rhs=G.t[:, i - 4, :], start=True, stop=False)
                return e.matmul(out, lhsT=kvk.t[rb:rb + 64, h // 2, 128 * i:128 * i + 128], rhs=self.qaT[1].t[rb:rb + 64, h // 2, :],
                                start=False, stop=True, tile_position=(rb, 0))

            def done(ob, h=h):
                self._epi_dense(ob, nq, h, g, h // 2, NP)
                after_unit(6.0)
            units.append(dict(nq=nq, n=12, pre=pre, score_fn=sc,
                              score_reads=lambda i, G=G, h=h: [self._kres(0, i), self.qaT_res[1][h // 2], G.res, self.identb.res],
                              v_fn=lambda i, h=h: self.kvv_lhsT(i, h), v_reads=lambda i: [self.kvv_res[i]], scale=1.0, done=done))
        self._attn_stream(units)
        self._load_kv(l, 1, 512, 1536, None, self.i_cvb[l])
        kvk, kvv = self.kvk[1], self.kvvf
        units = []
        for hp in range(2):
            opair = self.opair_t[self.o_rr % 2]
            self.o_rr += 1
            for h in (2 * hp, 2 * hp + 1):
                pu = []
                for c in range(2):
                    rb = 64 * (h % 2) + 32 * c

                    def sc(e, i, out, h=h, rb=rb, kvk=kvk):
                        return e.matmul(out, lhsT=kvk.t[rb:rb + 32, h // 2, 128 * i:128 * i + 128], rhs=self.qbT[1].t[rb:rb + 32, h // 2, :],
                                        start=True, stop=True, tile_position=(rb, 0))
                    pu.append(dict(nq=nq, n=12, score_fn=sc, score_reads=lambda i, h=h: [self._kres(1, i), self.qbT_res[1][h // 2]],
                                   v_fn=lambda i, h=h: self.kvv_lhsT(i, h), v_reads=lambda i: [self.kvv_res[i]], scale=32 ** -0.5,
                                   done=(lambda ob: after_unit(6.0))))

                def done(ob, h=h, hp=hp, pu=pu, opair=opair):
                    after_unit(6.0)
                    self._epi_diff_head(pu[0]["ob"], pu[1]["ob"], nq, h, l, opair)
                    if h == 2 * hp + 1:
                        self._epi_diff_pair(opair, nq, l, g, 2 + hp, NP)
                pu[1]["done"] = done
                units += pu
        self._attn_stream(units)
        self._load_kv(l, 0, 0, None, self.i_cckvT[l], None, ck2_src=self.i_ckpe4[l], kcols2=256)
        kvk, kvv = self.kvk[0], self.kvvf
        for i in range(12):
            ps = self.bank()
            self.pe(lambda e, ps=ps, i=i: e.matmul(ps.t[:, 0:256], lhsT=kvk.t[:, 0, 128 * i:128 * i + 128], rhs=self.wuv.t[:, l, :],
                                                  start=True, stop=True), [self._kres(0, i), self.wuv.res], [ps.res])
            self.cp(self.kvv_h(i)[:, :, 0:64], ps.t[:, 0:256].rearrange("p (h d) -> p h d", h=4), [ps.res], [self.kvv_res[i]], eng="dve")
        units = []
        for h in range(4):
            def sc(e, i, out, h=h, kvk=kvk):
                e.matmul(out, lhsT=kvk.t[:, 0, 128 * i:128 * i + 128], rhs=self.qabsT[1].t[:, h, :], start=True, stop=False)
                return e.matmul(out, lhsT=kvk.t[32 * h:32 * h + 32, 1, 128 * i:128 * i + 128], rhs=self.qpeT[1].t[32 * h:32 * h + 32, :],
                                start=False, stop=True, tile_position=(32 * h, 0))

            def done(ob, h=h):
                self._epi_dense(ob, nq, h, g, 4 + h // 2, NP)
                after_unit(6.0)
            units.append(dict(nq=nq, n=12, score_fn=sc, score_reads=lambda i: [self._kres(0, i), self.qabsT[1].res, self.qpeT[1].res],
                              v_fn=lambda i, h=h: self.kvv_lhsT(i, h), v_reads=lambda i: [self.kvv_res[i]], scale=96 ** -0.5, done=done))
        self._attn_stream(units)
        self._sgu(l, 1)

    def _out_proj(self, l):
        mod = self.mod[l]
        ws = [self.ring[(self.ring_used + jb) % RING] for jb in range(2)]

        def chunk(g, oc):
            c0, n, v = self.GR[g]
            w = ws[oc // 4]
            o4 = oc % 4
            ps = self.bank()

            def fn(e):
                ins = None
                for k in range(8):
                    ins = e.matmul(ps.t[:, :n], lhsT=w.t[:, 512 * k + 128 * o4:512 * k + 128 * o4 + 128], rhs=self.hT.t[:, k, c0:c0 + n],
                                   start=(k == 0), stop=(k == 7))
                return ins
            self.pe(fn, [w.res] + self.hres[g], [ps.res])
            xs = self.xT.t[:, oc, c0:c0 + n]
            self.stt(xs, ps.t[:, :n], mod.t[:, v, 2, oc:oc + 1], xs, ALU.mult, ALU.add,
                     [ps.res, self.mod_res[l][0], self.xres[g][oc]], [self.xres[g][oc]])
        for oc in range(8):
            chunk(1, oc)
        for oc in range(4):
            chunk(0, oc)
        self._norm(l, 1, 2)
        for oc in range(4, 8):
            chunk(0, oc)
        self._norm(l, 0, 2)
        self.ring_release()
        self.ring_release()

    def _ffn(self, l):
        mod = self.mod[l]
        TG = ((384, 768), (0, 384))

        def hreads(a, b):
            r = []
            if a < NP:
                r += self.hres[0]
            if b > NP:
                r += self.hres[1]
            return r
        for fb in range(8):
            w1 = self.ring_acquire()
            for (a, b) in TG:
                n = b - a
                for fc in range(4):
                    ps = self.bank()

                    def fn(e, ps=ps, fc=fc, a=a, b=b, n=n):
                        ins = None
                        for k in range(8):
                            ins = e.matmul(ps.t[:, :n], lhsT=w1.t[:, 512 * k + 128 * fc:512 * k + 128 * fc + 128], rhs=self.hT.t[:, k, a:b],
                                           start=(k == 0), stop=(k == 7))
                        return ins
                    self.pe(fn, [w1.res] + hreads(a, b), [ps.res])
                    r = self.ft()
                    self.act(r.t[:, :n], ps.t[:, :n], AF.Relu, [ps.res], [r.res])
                    tgi = 0 if a == 0 else 1
                    self.tt(self.uF(fc, a, b), r.t[:, :n], r.t[:, :n], ALU.mult, [r.res],
                            [self.uF_res[tgi][fc]] + (self.kvk_res[0] if fb == 0 else []))
            self.ring_release()
            w2 = self.ring_acquire()
            last = (fb == 7 and l + 1 < L and self.stage >= 99)

            def chunk(a, b, oc):
                n = b - a
                ps = self.bank()

                def fn(e):
                    ins = None
                    for fc in range(4):
                        ins = e.matmul(ps.t[:, :n], lhsT=w2.t[:, 1024 * fc + 128 * oc:1024 * fc + 128 * oc + 128], rhs=self.uF(fc, a, b),
                                       start=(fc == 0), stop=(fc == 3))
                    return ins
                tgi = 0 if a == 0 else 1
                self.pe(fn, [w2.res] + self.uF_res[tgi] + (self.kvk_res[0] if fb == 7 else []), [ps.res])
                for (g, lo, hi) in ((0, a, min(b, NP)), (1, max(a, NP), b)):
                    if hi <= lo:
                        continue
                    v = self.GR[g][2]
                    xs = self.xT.t[:, oc, lo:hi]
                    self.stt(xs, ps.t[:, lo - a:hi - a], mod.t[:, v, 5, oc:oc + 1], xs, ALU.mult, ALU.add,
                             [ps.res, self.mod_res[l][1], self.xres[g][oc]], [self.xres[g][oc]])
            for oc in range(8):
                chunk(384, 768, oc)
            for oc in range(4):
                chunk(0, 384, oc)
            if last:
                self._norm(l + 1, 1, 1)
            for oc in range(4, 8):
                chunk(0, 384, oc)
            if last:
                self._norm(l + 1, 0, 1)
                self.norm1_done = l + 1
            self.ring_release()

    def _layer(self, l):
        st = self.stage
        if st < 2:
            return
        if getattr(self, "norm1_done", -1) != l:
            for g in (1, 0):
                self._norm(l, g, 1)
        if st < 3:
            return
        self._in_proj(l)
        if st < 4:
            return
        todo = [(l + 1, j) for j in range(3)] if l + 1 < L else []
        state = {"u": 0.0}

        def do_one():
            ll, j = todo.pop(0)
            self._ada_block(ll, j)
            if j == 2:
                self._ada_exchange(ll)

        def after_unit(wt=1.0):
            state["u"] += wt
            while todo and state["u"] >= 4.0:
                state["u"] -= 4.0
                do_one()
        if st >= 5:
            self._load_kv(l, 0, 0, 1024, self.i_ckaT[l], self.i_cva[l])
            self._load_kv(l, 1, 512, None, self.i_ckbT[l], None)
        self._mix_prompt(l, after_unit)
        if st >= 5:
            self._mix_sample(l, after_unit)
        while todo:
            do_one()
        if l + 1 < L:
            self._ada_finish(l + 1, 0)
            self._ada_finish(l + 1, 1)
        if st < 6:
            return
        self._out_proj(l)
        if st < 7:
            return
        self._ffn(l)

    def _final(self):
        gf = [self.stg[0], self.stg[1]]
        for half in range(2):
            self.S.dma("sp", gf[half].t[:, :], self.i_gfin[:, 512 * half:512 * half + 512], writes=[gf[half].res])
        for g in range(2):
            c0, n, v = self.GR[g]
            dst = self.o_yp if g == 0 else self.o_ys
            for t in range(n // 128):
                a = c0 + 128 * t
                halves = []
                ss = self.smt()
                for half in range(2):
                    ps = self.bank()

                    def fn(e, ps=ps, half=half, a=a):
                        ins = None
                        for kk in range(4):
                            k = 4 * half + kk
                            ins = e.transpose(ps.t[:, 128 * kk:128 * kk + 128], self.xT.t[:, k, a:a + 128], self.ident.t[:, :])
                        return ins
                    self.pe(fn, [self.ident.res] + self.xres[g][4 * half:4 * half + 4], [ps.res])
                    xh = self.ft()
                    self.cp(xh.t[:, :], ps.t[:, :], [ps.res], [xh.res], eng="dve")
                    junk = self.bt()
                    self.act(junk.t[:, :], xh.t[:, :], AF.Square, [xh.res], [junk.res, ss.res], accum=ss.t[:, half:half + 1])
                    halves.append(xh)
                self.tt(ss.t[:, 2:3], ss.t[:, 0:1], ss.t[:, 1:2], ALU.add, [ss.res], [ss.res])
                self.act(ss.t[:, 3:4], ss.t[:, 2:3], AF.Ln, [ss.res, self.epsc.res], [ss.res], scale=1.0 / D, bias=self.epsc.t[:, :])
                self.act(ss.t[:, 4:5], ss.t[:, 3:4], AF.Exp, [ss.res], [ss.res], scale=-0.5)
                for half in range(2):
                    xh = halves[half]
                    self.stt(xh.t[:, :], xh.t[:, :], ss.t[:, 4:5], gf[half].t[:, :], ALU.mult, ALU.mult,
                             [xh.res, ss.res, gf[half].res], [xh.res])
                    self._out_dma(dst[128 * t:128 * t + 128, 512 * half:512 * half + 512], xh.t[:, :], xh.res)


OFF = dict(qa=0, ka=256, va=512, qb=768, kb=1024, vb=1280, qc=1536, ckv=1920, kpe=2048, u=2080, vs=2336)


def _swap_idx(n):
    i = np.arange(n)
    return np.where((i % 16) < 8, i + 8, i - 8)


def _blk8(wcols):
    n = wcols.shape[1]
    out = np.zeros((128, 8, 512), np.float32)
    out[:, :, :n] = wcols.reshape(8, 128, n).transpose(1, 0, 2)
    return out.reshape(128, 4096)


def _weight_blocks(inp):
    blocks = []
    w_ada, w_in, w_out, w_ff1, w_ff2 = (np.asarray(inp[k], np.float32) for k in ("w_ada", "w_in", "w_out", "w_ff1", "w_ff2"))

    def ada(l):
        return [_blk8(w_ada[l][:, 512 * j:512 * j + 512]) for j in range(12)]

    def inb(l):
        W = w_in[l]
        c = lambda name, n: W[:, OFF[name]:OFF[name] + n]
        qb, kb = c("qb", 256), c("kb", 256)
        qc = c("qc", 384).reshape(1024, 4, 96)
        qn = qc[:, :, :64].reshape(1024, 256)
        qpe = qc[:, :, 64:].reshape(1024, 128)
        kpe = c("kpe", 32)
        kpe4 = np.tile(kpe, (1, 4))
        sw256 = _swap_idx(256)
        sw128 = _swap_idx(128)
        return [
            _blk8(np.concatenate([c("u", 256), c("vs", 256)], 1)),
            _blk8(np.concatenate([c("qa", 256), c("ka", 256)], 1)),
            _blk8(np.concatenate([qb, qb[:, sw256]], 1)),
            _blk8(np.concatenate([kb, kb[:, sw256]], 1)),
            _blk8(np.concatenate([c("va", 256), c("vb", 256)], 1)),
            _blk8(np.concatenate([qn, c("ckv", 128)], 1)),
            _blk8(np.concatenate([qpe, qpe[:, sw128], kpe4, kpe4[:, sw128]], 1)),
        ]

    def outb(l):
        return [_blk8(w_out[l][:, 512 * j:512 * j + 512]) for j in range(2)]

    def ffnb(l):
        r = []
        for fb in range(8):
            r.append(_blk8(w_ff1[l][:, 512 * fb:512 * fb + 512]))
            w2 = w_ff2[l][512 * fb:512 * fb + 512, :].reshape(4, 128, 1024).transpose(1, 0, 2)
            r.append(np.ascontiguousarray(w2).reshape(128, 4096))
        return r

    for l in range(L):
        blocks += inb(l)
        blocks += outb(l)
        blocks += ffnb(l)
    assert len(blocks) == L * NBLK_L
    wada = [np.stack([ada(l)[3 * r + j] for l in range(L) for j in range(3)], 0) for r in range(4)]
    return np.stack(blocks, 0), wada


def _rope_tables(rank):
    f = np.arange(128) % 32
    is_col = f >= 16
    j = (f % 16) % 8
    first = (f % 16) < 8
    freqs = (np.float32(10000.0) ** (-np.arange(8, dtype=np.float32) / np.float32(8))).astype(np.float32)
    t = 256 * rank + np.arange(256)
    rows = (t // 64).astype(np.float32)
    cols = (t % 64).astype(np.float32)
    pos = np.where(is_col[:, None], cols[None, :], rows[None, :]).astype(np.float32)
    ang = (pos * freqs[j][:, None]).astype(np.float32)
    C = np.cos(ang).astype(np.float32)
    Sn = np.sin(ang).astype(np.float32)
    return C, np.where(first[:, None], -Sn, Sn).astype(np.float32)


def _gbias(rpb, rank):
    kr = np.arange(2)[:, None, None, None, None]
    kcol = np.arange(64)[None, :, None, None, None]
    c = np.arange(8)[None, None, :, None, None]
    qr = np.arange(4)[None, None, None, :, None]
    qcol = np.arange(64)[None, None, None, None, :]
    krow = 2 * c + kr
    qrow = 4 * rank + qr
    r0 = np.clip(qrow - 4, 0, 8)
    vrow = (krow >= r0) & (krow < r0 + 8)
    c0 = np.clip(qcol - 8, 0, 48)
    vcol = (kcol >= c0) & (kcol < c0 + 16)
    valid = np.broadcast_to(vrow & vcol, (2, 64, 8, 4, 64))
    di = np.broadcast_to(np.clip(krow - qrow + 7, 0, 14), (2, 64, 8, 4, 64))
    dj = np.broadcast_to(np.clip(kcol - qcol, -15, 15) + 15, (2, 64, 8, 4, 64))
    g = rpb[:, :, di, dj]
    g = np.where(valid[None, None], g, np.float32(NEG)).astype(np.float32)
    return g.reshape(L * 4, 128, 8 * 256)


def _prep_shared(inp):
    f = lambda k: np.asarray(inp[k], np.float32)
    sh = {}
    sh["wblk"], sh["_wada"] = _weight_blocks(inp)
    bada = f("b_ada").reshape(L, 48, 128).transpose(2, 0, 1)
    sh["bada"] = np.ascontiguousarray(bada).reshape(128, L * 48)
    sh["_badap"] = [np.ascontiguousarray(bada[:, :, 12 * r:12 * r + 12]).reshape(128, L * 12) for r in range(4)]
    sh["gmix"] = np.ascontiguousarray(f("g_mix").reshape(L, 8, 128).transpose(2, 0, 1)).reshape(128, L * 8)
    sh["gffn"] = np.ascontiguousarray(f("g_ffn").reshape(L, 8, 128).transpose(2, 0, 1)).reshape(128, L * 8)
    lqk = np.stack([f("diff_lq1"), f("diff_lq2"), f("diff_lk1"), f("diff_lk2")], 1)
    sh["lqk"] = np.ascontiguousarray(np.broadcast_to(lqk.reshape(1, -1), (128, L * 128)))
    gs = f("diff_g_subln")
    sh["gsub"] = np.ascontiguousarray(np.concatenate([gs, gs], 1).T)
    gk = f("mla_g_ckv")
    sh["gckvf"] = np.ascontiguousarray(gk.T)
    sh["gckvb"] = np.ascontiguousarray(np.broadcast_to(gk.reshape(1, -1), (128, L * 128)))
    wuk = f("mla_w_uk")
    wukT = wuk.reshape(L, 128, 2, 2, 64).transpose(3, 4, 0, 2, 1)
    sh["wukT"] = np.ascontiguousarray(wukT).reshape(128, L * 2 * 128)
    sh["wuv"] = np.ascontiguousarray(f("mla_w_uv").transpose(1, 0, 2)).reshape(128, L * 256)
    sh["sgug"] = np.ascontiguousarray(np.broadcast_to(f("sgu_g").reshape(1, -1), (128, L * 256)))
    sh["sguw"] = np.ascontiguousarray(f("sgu_w").transpose(3, 0, 1, 2)).reshape(128, L * 4 * 128)
    sb = f("sgu_b").reshape(L, 2, 2, 128)
    sb = np.broadcast_to(sb[:, :, :, None, :], (L, 2, 2, 64, 128)).transpose(2, 3, 0, 1, 4)
    sh["sgub"] = np.ascontiguousarray(sb).reshape(128, L * 2 * 128)
    sh["gfin"] = np.ascontiguousarray(np.broadcast_to(f("g_final").reshape(1, -1), (128, 1024)))
    sh["ident"] = np.eye(128, dtype=np.float32)
    return sh


def _prep_core(inp, c, sh, rank_cache):
    f = lambda k: np.asarray(inp[k], np.float32)
    sbatch, rank = c // 4, c % 4
    m = {k: v for k, v in sh.items() if not k.startswith("_")}
    m["wada"] = sh["_wada"][rank]
    m["badap"] = sh["_badap"][rank]
    xp = f("x_prompt")[2 * c:2 * c + 2].reshape(NP, D)
    xs = f("x_sample")[sbatch, 256 * rank:256 * rank + 256]
    x = np.concatenate([xp, xs], 0)
    m["xT"] = np.ascontiguousarray(x.T.reshape(8, 128, NT).transpose(1, 0, 2)).reshape(128, 8 * NT)
    mv = np.stack([f("c_ctx"), f("c")[sbatch]], 1)
    m["mT"] = np.ascontiguousarray(mv.reshape(8, 128, 2).transpose(1, 0, 2)).reshape(128, 16)
    if rank not in rank_cache:
        C, Sg = _rope_tables(rank)
        rank_cache[rank] = (C, Sg, _gbias(f("na_rpb"), rank))
    m["ropeC"], m["ropeS"], m["gbias"] = rank_cache[rank]
    key = ("b", sbatch)
    if key not in rank_cache:
        def kT(a):
            return np.ascontiguousarray(a.reshape(L, 2, 2, 512, 64).transpose(0, 2, 4, 1, 3)).reshape(L, 128, 1024)

        def vT(a):
            return np.ascontiguousarray(a.reshape(L, 4, 4, 128, 64).transpose(0, 3, 2, 1, 4)).reshape(L, 128, 1024)
        ckv = f("cache_mla_ckv")[sbatch]
        kpe = f("cache_mla_kpe")[sbatch]
        rank_cache[key] = dict(
            ckaT=kT(f("cache_na_k")[sbatch]), ckbT=kT(f("cache_diff_k")[sbatch]),
            cva=vT(f("cache_na_v")[sbatch]), cvb=vT(f("cache_diff_v")[sbatch]),
            cckvT=np.ascontiguousarray(ckv.transpose(0, 2, 1)),
            ckpe4=np.ascontiguousarray(np.tile(kpe.transpose(0, 2, 1), (1, 4, 1))))
    m.update(rank_cache[key])
    return m


_NC_CACHE = {}


def kernel(**inputs):
    if "nc" not in _NC_CACHE:
        _NC_CACHE["nc"] = Prog().build()
    nc = _NC_CACHE["nc"]
    sh = _prep_shared(inputs)
    cache = {}
    in_maps = [_prep_core(inputs, c, sh, cache) for c in range(8)]
    res = run_bass_kernel_spmd(nc, in_maps, core_ids=list(range(8)))
    R = res.results
    y_prompt = np.stack([R[c]["yp"].reshape(2, 256, D) for c in range(8)], 0).reshape(16, 256, D)
    y_sample = np.stack([np.concatenate([R[4 * b + r]["ys"] for r in range(4)], 0) for b in range(2)], 0)
    cat = lambda k: np.concatenate([R[c][k] for c in range(8)], 0)
    outs = (y_prompt, y_sample, cat("onak"), cat("onav"), cat("odk"), cat("odv"), cat("ockv"), cat("okpe"))
    return tuple(np.ascontiguousarray(o, dtype=np.float32) for o in outs)
```
